# Optimizing a Trainium2 kernel written in Bass

```python
import math
import jax, jax.numpy as jnp
from jax import lax
import numpy as np

D_MODEL = 2048
BATCH = 4
SEQ = 4096
DEPTH = 2

DILATED_GROUPS = ((128, 1), (512, 4), (2048, 16))
N_ATTN_GROUPS = len(DILATED_GROUPS)
HEADS_PER_GROUP = 4
HEAD_DIM = 128
N_ATTN_HEADS = N_ATTN_GROUPS * HEADS_PER_GROUP
QKV_W = N_ATTN_HEADS * HEAD_DIM
ATTN_OUT = HEADS_PER_GROUP * HEAD_DIM
REL_BUCKETS = 32
REL_MAX_DIST = 2048
D_INNER = 2 * D_MODEL
SSM_HEAD_DIM = 64
SSM_HEADS = D_INNER // SSM_HEAD_DIM
SSM_GROUPS = 8
HEADS_PER_SSM_GROUP = SSM_HEADS // SSM_GROUPS
D_STATE = 128
CONV_K = 4
CHUNK = 128
CONV_DIM = D_INNER + 2 * SSM_GROUPS * D_STATE
D_FF = ((8 * D_MODEL // 3 + 255) // 256) * 256
SPLITS = (QKV_W, QKV_W, QKV_W, D_INNER, CONV_DIM, SSM_HEADS, D_MODEL, D_MODEL)
N_IN = sum(SPLITS)
EPS = 1e-6

kernel_name = "hybrid_dilated_attn_ssd_gated_block"


def rmsnorm(x, w):
    x32 = x.astype(jnp.float32)
    y = x32 * lax.rsqrt(jnp.mean(x32 * x32, axis=-1, keepdims=True) + EPS)
    return (y * w.astype(jnp.float32)).astype(x.dtype)


def t5_bucket(dist):
    exact = REL_BUCKETS // 2
    n = jnp.maximum(dist, 1).astype(jnp.float32)
    large = exact + (jnp.log(n / exact) / math.log(REL_MAX_DIST / exact)
                     * (REL_BUCKETS - exact)).astype(jnp.int32)
    large = jnp.minimum(large, REL_BUCKETS - 1)
    return jnp.where(dist < exact, dist, large)


def dilated_window_attention(q, k, v, bias_table, window, dilation):
    B, S, Hg, Dh = q.shape
    blk = window // dilation
    unit = blk * dilation
    Sp = -(-S // unit) * unit
    L = Sp // dilation
    nb = L // blk

    def to_blocks(t):
        t = jnp.pad(t, ((0, 0), (0, Sp - S), (0, 0), (0, 0)))
        t = t.reshape(B, L, dilation, Hg, Dh).transpose(0, 2, 1, 3, 4)
        return t.reshape(B, dilation, nb, blk, Hg, Dh)

    qb, kb, vb = to_blocks(q), to_blocks(k), to_blocks(v)

    def with_prev(t):
        prev = jnp.pad(t[:, :, :-1], ((0, 0), (0, 0), (1, 0), (0, 0), (0, 0), (0, 0)))
        return jnp.concatenate([prev, t], axis=3)

    kk, vv = with_prev(kb), with_prev(vb)
    logits = jnp.einsum('brnqhd,brnkhd->brnhqk', qb, kk).astype(jnp.float32) / math.sqrt(Dh)

    qi = jnp.arange(blk)[:, None]
    kj = jnp.arange(2 * blk)[None, :]
    steps = blk + qi - kj
    band = (steps >= 0) & (steps <= blk)
    first = (jnp.arange(nb) == 0)[:, None, None]
    valid = band[None] & ~(first & (kj < blk)[None])
    bucket = t5_bucket(jnp.clip(steps, 0, blk) * dilation)
    bias = bias_table[bucket].astype(jnp.float32).transpose(2, 0, 1)

    logits = jnp.where(valid[None, None, :, None], logits + bias[None, None, None], -jnp.inf)
    m = jnp.max(logits, axis=-1, keepdims=True)
    p = jnp.exp(logits - m)
    s = jnp.sum(p, axis=-1, keepdims=True)
    o = jnp.einsum('brnhqk,brnkhd->brnqhd', (p / s).astype(v.dtype), vv)
    lse = (m + jnp.log(s))[..., 0]

    o = o.reshape(B, dilation, L, Hg, Dh).transpose(0, 2, 1, 3, 4).reshape(B, Sp, Hg, Dh)[:, :S]
    lse = lse.transpose(0, 1, 2, 4, 3).reshape(B, dilation, L, Hg)
    lse = lse.transpose(0, 2, 1, 3).reshape(B, Sp, Hg)[:, :S]
    return o, lse


def ssd_chunked(xs, dt, A, Bm, Cm):
    Bsz, S, H, P = xs.shape
    G, N = Bm.shape[2], Bm.shape[3]
    R = H // G
    nc = S // CHUNK
    f32 = jnp.float32
    x = xs.astype(f32).reshape(Bsz, nc, CHUNK, G, R, P)
    dtc = dt.astype(f32).reshape(Bsz, nc, CHUNK, G, R)
    Bc = Bm.astype(f32).reshape(Bsz, nc, CHUNK, G, N)
    Cc = Cm.astype(f32).reshape(Bsz, nc, CHUNK, G, N)
    acum = jnp.cumsum(dtc * A.astype(f32).reshape(G, R), axis=2)
    xdt = x * dtc[..., None]

    at = jnp.moveaxis(acum, 2, -1)
    seg = at[..., :, None] - at[..., None, :]
    tril = jnp.tril(jnp.ones((CHUNK, CHUNK), dtype=bool))
    Ldec = jnp.exp(jnp.where(tril, seg, -jnp.inf))
    cb = jnp.einsum('bclgn,bcsgn->bcgls', Cc, Bc)
    y_diag = jnp.einsum('bcgrls,bcsgrp->bclgrp', cb[:, :, :, None] * Ldec, xdt)

    decay_states = jnp.exp(acum[:, :, -1:] - acum)
    states = jnp.einsum('bclgn,bclgrp->bcgrpn', Bc, xdt * decay_states[..., None])
    chunk_decay = jnp.exp(acum[:, :, -1])

    def step(h, inp):
        st, dec = inp
        return h * dec[..., None, None] + st, h

    h0 = jnp.zeros_like(states[:, 0])
    _, prev = lax.scan(step, h0, (jnp.moveaxis(states, 1, 0), jnp.moveaxis(chunk_decay, 1, 0)))
    prev = jnp.moveaxis(prev, 0, 1)

    y_off = jnp.einsum('bclgn,bcgrpn,bclgr->bclgrp', Cc, prev, jnp.exp(acum))
    return (y_diag + y_off).reshape(Bsz, S, H, P)


def causal_depthwise_conv(u, w, b):
    S = u.shape[1]
    up = jnp.pad(u, ((0, 0), (CONV_K - 1, 0), (0, 0)))
    out = b
    for j in range(CONV_K):
        out = out + up[:, j:j + S] * w[j]
    return out


def mixer(h, rel_bias, w_in, conv_w, conv_b, dt_bias, a_log, d_skip, ssm_norm_w,
          w_attn_proj, w_ssm_proj, w_out):
    B, S, _ = h.shape
    proj = h @ w_in
    q, k, v, z, xbc, dt_raw, g_attn, g_ssm = jnp.split(
        proj, np.cumsum(SPLITS)[:-1].tolist(), axis=-1)

    q = q.reshape(B, S, N_ATTN_GROUPS, HEADS_PER_GROUP, HEAD_DIM)
    k = k.reshape(B, S, N_ATTN_GROUPS, HEADS_PER_GROUP, HEAD_DIM)
    v = v.reshape(B, S, N_ATTN_GROUPS, HEADS_PER_GROUP, HEAD_DIM)
    outs, lses = [], []
    for g, (window, dilation) in enumerate(DILATED_GROUPS):
        o, lse = dilated_window_attention(
            q[:, :, g], k[:, :, g], v[:, :, g],
            rel_bias[:, g * HEADS_PER_GROUP:(g + 1) * HEADS_PER_GROUP], window, dilation)
        outs.append(o)
        lses.append(lse)
    wts = jax.nn.softmax(jnp.stack(lses, axis=0), axis=0)
    attn = jnp.sum(wts[..., None] * jnp.stack(outs, axis=0).astype(jnp.float32), axis=0)
    attn = attn.astype(h.dtype).reshape(B, S, ATTN_OUT)

    xbc = jax.nn.silu(causal_depthwise_conv(xbc, conv_w, conv_b))
    xs, Bm, Cm = jnp.split(xbc, [D_INNER, D_INNER + SSM_GROUPS * D_STATE], axis=-1)
    xs = xs.reshape(B, S, SSM_HEADS, SSM_HEAD_DIM)
    dt = jax.nn.softplus(dt_raw.astype(jnp.float32) + dt_bias.astype(jnp.float32))
    A = -jnp.exp(a_log.astype(jnp.float32))
    y = ssd_chunked(xs, dt, A, Bm.reshape(B, S, SSM_GROUPS, D_STATE),
                    Cm.reshape(B, S, SSM_GROUPS, D_STATE))
    y = y + d_skip.astype(jnp.float32)[:, None] * xs.astype(jnp.float32)
    y = y.reshape(B, S, D_INNER) * jax.nn.silu(z.astype(jnp.float32))
    y = y.reshape(B, S, SSM_GROUPS, D_INNER // SSM_GROUPS)
    y = y * lax.rsqrt(jnp.mean(y * y, axis=-1, keepdims=True) + EPS)
    y = (y.reshape(B, S, D_INNER) * ssm_norm_w.astype(jnp.float32)).astype(h.dtype)

    merged = jax.nn.sigmoid(g_attn) * (attn @ w_attn_proj) + jax.nn.sigmoid(g_ssm) * (y @ w_ssm_proj)
    return merged @ w_out


def swiglu(h, w_ffn_in, w_ffn_out):
    hg, hu = jnp.split(h @ w_ffn_in, 2, axis=-1)
    return (jax.nn.silu(hg) * hu) @ w_ffn_out


def setup_inputs(seed: int = 0) -> dict:
    key = jax.random.key(seed)
    ks = jax.random.split(key, 24)
    f32 = jnp.float32

    def nrm(k, shape, s):
        return jax.random.normal(k, shape, f32) * s

    dt0 = jnp.exp(jax.random.uniform(ks[10], (DEPTH, SSM_HEADS), f32)
                  * (math.log(0.1) - math.log(0.001)) + math.log(0.001))
    return {
        "x": nrm(ks[0], (BATCH, SEQ, D_MODEL), 1.0),
        "c": nrm(ks[1], (BATCH, D_MODEL), 1.0),
        "rel_bias": nrm(ks[2], (REL_BUCKETS, N_ATTN_HEADS), 0.5),
        "norm1_w": 1.0 + nrm(ks[3], (DEPTH, D_MODEL), 0.05),
        "norm2_w": 1.0 + nrm(ks[4], (DEPTH, D_MODEL), 0.05),
        "w_mod": nrm(ks[5], (DEPTH, D_MODEL, 6 * D_MODEL), D_MODEL ** -0.5),
        "b_mod": nrm(ks[6], (DEPTH, 6 * D_MODEL), 0.01),
        "w_in": nrm(ks[7], (DEPTH, D_MODEL, N_IN), D_MODEL ** -0.5),
        "conv_w": nrm(ks[8], (DEPTH, CONV_K, CONV_DIM), CONV_K ** -0.5),
        "conv_b": nrm(ks[9], (DEPTH, CONV_DIM), 0.01),
        "dt_bias": dt0 + jnp.log(-jnp.expm1(-dt0)),
        "a_log": jnp.log(jax.random.uniform(ks[11], (DEPTH, SSM_HEADS), f32, 1.0, 16.0)),
        "d_skip": 1.0 + nrm(ks[12], (DEPTH, SSM_HEADS), 0.1),
        "ssm_norm_w": 1.0 + nrm(ks[13], (DEPTH, D_INNER), 0.05),
        "w_attn_proj": nrm(ks[14], (DEPTH, ATTN_OUT, D_MODEL), ATTN_OUT ** -0.5),
        "w_ssm_proj": nrm(ks[15], (DEPTH, D_INNER, D_MODEL), D_INNER ** -0.5),
        "w_out": nrm(ks[16], (DEPTH, D_MODEL, D_MODEL), D_MODEL ** -0.5),
        "w_ffn_in": nrm(ks[17], (DEPTH, D_MODEL, 2 * D_FF), D_MODEL ** -0.5),
        "w_ffn_out": nrm(ks[18], (DEPTH, D_FF, D_MODEL), D_FF ** -0.5),
        "final_norm_w": 1.0 + nrm(ks[19], (D_MODEL,), 0.05),
    }


def reference(x, c, rel_bias, norm1_w, norm2_w, w_mod, b_mod, w_in, conv_w, conv_b,
              dt_bias, a_log, d_skip, ssm_norm_w, w_attn_proj, w_ssm_proj, w_out,
              w_ffn_in, w_ffn_out, final_norm_w):
    c_act = jax.nn.silu(c)
    for l in range(DEPTH):
        mod = (c_act @ w_mod[l] + b_mod[l])[:, None, :]
        sh1, sc1, g1, sh2, sc2, g2 = jnp.split(mod, 6, axis=-1)
        h = rmsnorm(x, norm1_w[l]) * (1.0 + sc1) + sh1
        x = x + g1 * mixer(h, rel_bias, w_in[l], conv_w[l], conv_b[l], dt_bias[l], a_log[l],
                           d_skip[l], ssm_norm_w[l], w_attn_proj[l], w_ssm_proj[l], w_out[l])
        h = rmsnorm(x, norm2_w[l]) * (1.0 + sc2) + sh2
        x = x + g2 * swiglu(h, w_ffn_in[l], w_ffn_out[l])
    return rmsnorm(x, final_norm_w)
```

```python
import math
from contextlib import ExitStack
import numpy as np
import concourse.bass as bass
import concourse.mybir as mybir
from concourse.bass_utils import run_bass_kernel_spmd

F32 = mybir.dt.float32
BF16 = mybir.dt.bfloat16
AF = mybir.ActivationFunctionType
ALU = mybir.AluOpType
AX = mybir.AxisListType

D = 2048
S = 4096
DEPTH = 2
NIN = 19008
DFF = 5632
EPS = 1e-6
GROUPS = ((128, 1), (512, 4), (2048, 16))
NEG = -30000.0

ENGS = ["pe", "act", "dve", "pool", "sp"]
NDMA = 24
SAME_ENGINE_SYNC = True


class Prog:
    def __init__(self, nc, es):
        self.nc = nc
        self.sem = {e: es.enter_context(nc.semaphore("s_" + e)) for e in ENGS}
        self.cnt = {e: 0 for e in ENGS}
        self.dsem = [es.enter_context(nc.semaphore("d%d" % i)) for i in range(NDMA)]
        self.dcnt = [0] * NDMA
        self.drr = 0
        self.streams = {e: [] for e in ENGS}
        self.waited = {e: {} for e in ENGS}
        self.res = {}
        self.ninst = 0
        self.nops = 0
        import os
        self.maxops = int(os.environ.get("KMAXOPS", "0")) or None

    def _semof(self, key):
        return self.sem[key[1]] if key[0] == "e" else self.dsem[key[1]]

    def op(self, eng, fns, reads=(), writes=(), dma=False):
        if not isinstance(fns, (list, tuple)):
            fns = [fns]
        self.nops += 1
        if self.maxops is not None and self.nops > self.maxops:
            return None
        deps = {}

        def add(tok):
            if tok is None:
                return
            k, v = tok
            if deps.get(k, 0) < v:
                deps[k] = v

        for r in reads:
            st = self.res.get(r)
            if st is not None:
                add(st[0])
        for w in writes:
            st = self.res.get(w)
            if st is not None:
                add(st[0])
                for k, v in st[1].items():
                    add((k, v))
        if dma:
            i = self.drr
            self.drr = (self.drr + 1) % NDMA
            if self.dcnt[i] > 0:
                add((("d", i), self.dcnt[i]))
            self.dcnt[i] += 16
            tok = (("d", i), self.dcnt[i])
            inc = (self.dsem[i], 16)
        else:
            self.cnt[eng] += 1
            tok = (("e", eng), self.cnt[eng])
            inc = (self.sem[eng], 1)
        waits = []
        wd = self.waited[eng]
        for k, v in deps.items():
            if k == ("e", eng) and (eng == "pe" or not SAME_ENGINE_SYNC):
                continue
            if wd.get(k, 0) >= v:
                continue
            wd[k] = v
            waits.append((self._semof(k), v))
        self.streams[eng].append((waits, list(fns), inc))
        self.ninst += len(fns)
        for w in writes:
            self.res[w] = [tok, {}]
        for r in reads:
            st = self.res.get(r)
            if st is None:
                st = self.res[r] = [None, {}]
            k, v = tok
            if st[1].get(k, 0) < v:
                st[1][k] = v
        return tok

    def flush(self):
        nc = self.nc
        for i in range(NDMA):
            if self.dcnt[i] > 0 and self.waited["sp"].get(("d", i), 0) < self.dcnt[i]:
                self.waited["sp"][("d", i)] = self.dcnt[i]
                self.streams["sp"].append(([(self.dsem[i], self.dcnt[i])], [], None))
        for e in ENGS:
            if e == "sp":
                continue
            if self.cnt[e] > 0 and self.waited["sp"].get(("e", e), 0) < self.cnt[e]:
                self.waited["sp"][("e", e)] = self.cnt[e]
                self.streams["sp"].append(([(self.sem[e], self.cnt[e])], [], None))
        streams = self.streams
        self.streams = {e: [] for e in ENGS}
        for e in ENGS:
            for e2 in ENGS:
                self.waited[e][("e", e2)] = self.cnt[e2]
            for i in range(NDMA):
                self.waited[e][("d", i)] = self.dcnt[i]
        self.res = {}

        def replay(engobj, lst):
            for waits, fns, inc in lst:
                for s, v in waits:
                    engobj.wait_ge(s, v)
                ins = None
                for f in fns:
                    ins = f(engobj)
                if inc is not None and ins is not None:
                    ins.then_inc(inc[0], inc[1])

        with nc.Block() as block:
            @block.tensor
            def _(e):
                replay(e, streams["pe"])

            @block.scalar
            def _(e):
                replay(e, streams["act"])

            @block.vector
            def _(e):
                replay(e, streams["dve"])

            @block.gpsimd
            def _(e):
                replay(e, streams["pool"])

            @block.sync
            def _(e):
                replay(e, streams["sp"])


def t5_bucket_np(dist):
    exact = 16
    n = np.maximum(dist, 1).astype(np.float32)
    large = exact + (np.log(n / np.float32(exact)) / np.float32(math.log(2048 / exact)) * np.float32(32 - exact)).astype(np.int32)
    large = np.minimum(large, 31)
    return np.where(dist < exact, dist, large)


def host_consts():
    c = {}
    i = np.arange(128)
    c["tri"] = (i[:, None] <= i[None, :]).astype(np.float32)
    c["gst"] = (i[:, None] > i[None, :]).astype(np.float32)
    c["ones"] = np.ones((128, 128), np.float32)
    c["ident"] = np.eye(128, dtype=np.float32)
    oh = np.zeros((3, 33, 384), np.float32)
    for g, (win, dil) in enumerate(GROUPS):
        for ii in range(384):
            steps = 255 - ii
            if 0 <= steps <= 128:
                b = int(t5_bucket_np(np.array([steps * dil]))[0])
                oh[g, b, ii] = 1.0
            else:
                oh[g, 32, ii] = NEG
    c["ohu"] = oh
    return c


def build(dbg=(), nlayers=DEPTH, only=None, feed=(), skip=()):
    nc = bass.Bass("TRN2", target_bir_lowering=False)
    dbg = set(dbg)
    uid = [0]

    def SBT(name, shape, dt):
        uid[0] += 1
        return nc.sbuf_tensor("%s_%d" % (name, uid[0]), list(shape), dt)

    def PST(name, shape, dt):
        uid[0] += 1
        return nc.psum_tensor("%s_%d" % (name, uid[0]), list(shape), dt)

    feed = set(feed)
    skip = set(skip)

    def run(stage):
        return only is None or stage in only

    def din(name, shape, dt=F32):
        if name in skip:
            return nc.dram_tensor(name, [2, 2, 2], dt, kind="Internal").ap()
        return nc.dram_tensor(name, list(shape), dt, kind="ExternalInput").ap()

    def dscr(name, shape, dt):
        kind = "ExternalOutput" if name in dbg else ("ExternalInput" if name in feed else "Internal")
        return nc.dram_tensor(name, list(shape), dt, kind=kind).ap()

    xT_in = din("xT", [D, S])
    cT_in = din("cT", [128, 16])
    relb_in = din("rel_bias", [32, 12])
    n1w_in = din("n1w", [DEPTH, 128, 16])
    n2w_in = din("n2w", [DEPTH, 128, 16])
    fnw_in = din("fnw", [128, 16])
    wmod_in = din("w_mod", [DEPTH, D, 6 * D])
    bmod_in = din("bmodT", [DEPTH, 128, 96])
    win_in = din("w_in", [DEPTH, D, NIN])
    convw_in = din("convwT", [DEPTH, 128, 48, 4])
    convb_in = din("convbT", [DEPTH, 128, 48])
    dtb_in = din("dtb_b", [DEPTH, 128, 64])
    alog_in = din("alog_b", [DEPTH, 128, 64])
    dsk_in = din("dsk_b", [DEPTH, 128, 4096])
    snw_in = din("snwT", [DEPTH, 128, 32])
    wattn_in = din("w_attn_proj", [DEPTH, 512, D])
    wssm_in = din("w_ssm_proj", [DEPTH, 4096, D])
    wout_in = din("w_out", [DEPTH, D, D])
    wffi_in = din("w_ffn_in", [DEPTH, D, 2 * DFF])
    wffo_in = din("w_ffn_out", [DEPTH, DFF, D])
    tri_in = din("tri", [128, 128])
    gst_in = din("gst", [128, 128])
    ones_in = din("ones", [128, 128])
    ident_in = din("ident", [128, 128])
    ohu_in = din("ohu", [3, 33, 384])
    outT = nc.dram_tensor("outT", [D, S], F32, kind="ExternalOutput").ap()

    xs_d = dscr("xs", [D, S], F32)
    qT_d = dscr("qT", [1536, S], BF16)
    kT_d = dscr("kT", [1536, S], BF16)
    v_d = dscr("v", [S, 1536], BF16)
    z_d = dscr("z", [S, 4096], BF16)
    xbcT_d = dscr("xbcT", [6144, S], BF16)
    dtr_d = dscr("dtr", [S, 64], F32)
    gaT_d = dscr("gaT", [D, S], BF16)
    gsT_d = dscr("gsT", [D, S], BF16)
    ao_d = dscr("ao", [3, S, 4 * 129], F32)
    xst_d = dscr("xst", [S, 4096], BF16)
    bt_d = dscr("btok", [S, 1024], BF16)
    BT_d = dscr("BT", [1024, S], BF16)
    CT_d = dscr("CT", [1024, S], BF16)
    yT_d = dscr("yT", [4096, S], BF16)
    toep_t = nc.dram_tensor("toep", [12, 128 * 384], F32)
    toep_d = toep_t.ap()
    hT_dbg = dscr("hTd", [D, S], BF16) if "hTd" in dbg else None
    mod_dbg = dscr("modd", [DEPTH, 128, 96], F32) if "modd" in dbg else None

    with ExitStack() as es:
        P = Prog(nc, es)

        def sbp(name, shape, dt):
            return es.enter_context(SBT("sb_" + name, list(shape), dt))

        tri = sbp("tri", [128, 128], F32)
        gst = sbp("gst", [128, 128], F32)
        ones = sbp("ones", [128, 128], F32)
        identf = sbp("identf", [128, 128], F32)
        ident = sbp("ident", [128, 128], BF16)
        modT = sbp("modT", [128, DEPTH, 96], F32)
        a1 = sbp("a1", [128, DEPTH, 16], F32)
        a2 = sbp("a2", [128, DEPTH, 16], F32)
        n1w = sbp("n1w", [128, DEPTH, 16], F32)
        n2w = sbp("n2w", [128, DEPTH, 16], F32)
        fnw = sbp("fnw", [128, 16], F32)
        zero16 = sbp("zero16", [128, 16], F32)
        cact = sbp("cact", [128, 16], F32)

        for t, src, key in ((tri, tri_in, "tri"), (gst, gst_in, "gst"), (ones, ones_in, "ones"), (identf, ident_in, "identf"), (fnw, fnw_in, "fnw"), (cact, cT_in, "cact")):
            P.op("sp", lambda e, t=t, src=src: e.dma_start(out=t[:], in_=src), writes=[key], dma=True)
        P.op("sp", lambda e: e.dma_start(out=n1w[:], in_=n1w_in.rearrange("l p i -> p l i")), writes=["n1w"], dma=True)
        P.op("sp", lambda e: e.dma_start(out=n2w[:], in_=n2w_in.rearrange("l p i -> p l i")), writes=["n2w"], dma=True)
        P.op("dve", lambda e: e.tensor_copy(out=ident[:], in_=identf[:]), reads=["identf"], writes=["ident"])
        P.op("dve", lambda e: e.memset(zero16[:], 0.0), writes=["zero16"])
        P.op("act", lambda e: e.activation(out=cact[:], in_=cact[:], func=AF.Silu), reads=["cact"], writes=["cact"])

        if not run("M"):
            P.flush()
        with ExitStack() as ms:
            if not run("M"):
                nlm = 0
            else:
                nlm = nlayers
            wm = [ms.enter_context(SBT("wm%d" % i, [128, 16, 512], F32)) for i in range(2)]
            bm = ms.enter_context(SBT("bm", [128, DEPTH, 96], F32))
            psm = ms.enter_context(PST("psm", [128, DEPTH * 96], F32))
            P.op("sp", lambda e: e.dma_start(out=bm[:], in_=bmod_in.rearrange("l p j -> p l j")), writes=["bm"], dma=True)
            it = 0
            for l in range(nlm):
                for gcol in range(24):
                    wt = wm[it % 2]
                    for hh in range(2):
                        P.op("sp", lambda e, wt=wt, l=l, gcol=gcol, hh=hh: e.dma_start(
                            out=wt[:, hh * 8:(hh + 1) * 8, :],
                            in_=wmod_in[l, hh * 1024:(hh + 1) * 1024, gcol * 512:(gcol + 1) * 512].rearrange("(c p) n -> p c n", p=128)),
                            writes=[(wt.name, hh)], dma=True)
                    fns = []
                    for blk in range(4):
                        j = gcol * 4 + blk
                        for kc in range(16):
                            fns.append(lambda e, wt=wt, blk=blk, kc=kc, j=j, l=l: e.matmul(
                                psm[:, l * 96 + j:l * 96 + j + 1], lhsT=wt[:, kc, blk * 128:(blk + 1) * 128], rhs=cact[:, kc:kc + 1],
                                start=(kc == 0), stop=(kc == 15)))
                    P.op("pe", fns, reads=[(wt.name, 0), (wt.name, 1), "cact"], writes=["psm"])
                    it += 1
                P.op("dve", lambda e, l=l: e.tensor_tensor(out=modT[:, l, :], in0=psm[:, l * 96:(l + 1) * 96], in1=bm[:, l, :], op=ALU.add),
                     reads=["psm", "bm"], writes=["modT"])
                P.op("dve", lambda e, l=l: e.scalar_tensor_tensor(out=a1[:, l, :], in0=modT[:, l, 16:32], scalar=1.0, in1=n1w[:, l, :], op0=ALU.add, op1=ALU.mult),
                     reads=["modT", "n1w"], writes=["a1"])
                P.op("dve", lambda e, l=l: e.scalar_tensor_tensor(out=a2[:, l, :], in0=modT[:, l, 64:80], scalar=1.0, in1=n2w[:, l, :], op0=ALU.add, op1=ALU.mult),
                     reads=["modT", "n2w"], writes=["a2"])
            if mod_dbg is not None:
                P.op("sp", lambda e: e.dma_start(out=mod_dbg.rearrange("l p j -> p l j"), in_=modT[:]), reads=["modT"], dma=True)
            P.flush()

        def norm_tiles(src_d, avec, bvec, dst_fn, tok0, ntok, pools, tag):
            xt, sq, rsb, tmpb, psn = pools
            nt = ntok // 256
            for ti in range(nt):
                t0 = tok0 + ti * 256
                xb = xt[ti % 2]
                for hh in range(2):
                    P.op("sp", lambda e, xb=xb, hh=hh, t0=t0: e.dma_start(
                        out=xb[:, hh * 8:(hh + 1) * 8, :], in_=src_d[hh * 1024:(hh + 1) * 1024, t0:t0 + 256].rearrange("(c p) t -> p c t", p=128)),
                        writes=[(xb.name, hh)], dma=True)
                pn = psn[ti % 2]
                for i in range(16):
                    sqb = sq[i % 2]
                    P.op("act", lambda e, sqb=sqb, xb=xb, i=i: e.activation(out=sqb[:], in_=xb[:, i, :], func=AF.Square),
                         reads=[(xb.name, i // 8)], writes=[sqb.name])
                    P.op("pe", lambda e, sqb=sqb, pn=pn, i=i: e.matmul(pn[:], lhsT=ones[:], rhs=sqb[:], start=(i == 0), stop=(i == 15)),
                         reads=[sqb.name, "ones"], writes=[pn.name])
                rs = rsb[ti % 2]
                P.op("dve", lambda e, rs=rs, pn=pn: e.tensor_scalar(out=rs[:], in0=pn[:], scalar1=1.0 / D, scalar2=EPS, op0=ALU.mult, op1=ALU.add),
                     reads=[pn.name], writes=[rs.name])
                P.op("act", lambda e, rs=rs: e.activation(out=rs[:], in_=rs[:], func=AF.Sqrt), reads=[rs.name], writes=[rs.name])
                P.op("dve", lambda e, rs=rs: e.reciprocal(out=rs[:], in_=rs[:]), reads=[rs.name], writes=[rs.name])
                for i in range(16):
                    tb = tmpb[i % 2]
                    P.op("dve", lambda e, tb=tb, xb=xb, rs=rs, i=i: e.tensor_tensor(out=tb[:], in0=xb[:, i, :], in1=rs[:], op=ALU.mult),
                         reads=[(xb.name, i // 8), rs.name], writes=[tb.name])
                    dst, dres = dst_fn(i, t0, 256)
                    P.op("act", lambda e, tb=tb, dst=dst, i=i: e.activation(out=dst, in_=tb[:], func=AF.Identity, bias=bvec[:, i:i + 1], scale=avec[:, i:i + 1]),
                         reads=[tb.name, "a1", "a2", "modT"], writes=[dres])

        def norm_pools(st):
            xt = [st.enter_context(SBT("nx%d" % i, [128, 16, 256], F32)) for i in range(2)]
            sq = [st.enter_context(SBT("nsq%d" % i, [128, 256], F32)) for i in range(2)]
            rsb = [st.enter_context(SBT("nrs%d" % i, [128, 256], F32)) for i in range(2)]
            tmpb = [st.enter_context(SBT("ntm%d" % i, [128, 256], F32)) for i in range(2)]
            psn = [st.enter_context(PST("psn%d" % i, [128, 256], F32)) for i in range(2)]
            return xt, sq, rsb, tmpb, psn

        def stage_B(l):
            with ExitStack() as st:
                def sb(name, shape, dt):
                    return st.enter_context(SBT("B_" + name, list(shape), dt))

                def pp(name, shape, dt):
                    return st.enter_context(PST("Bp_" + name, list(shape), dt))
                psU = pp("U", [128, 512], F32)
                psS = [pp("S%d" % i, [128, 256], F32) for i in range(2)]
                psT = [pp("T%d" % i, [128, 256], BF16) for i in range(2)]
                psO = [pp("O%d" % i, [128, 128], F32) for i in range(2)]
                QK = [sb("qk%d" % i, [128, S], BF16) for i in range(8)]
                Vg = sb("Vg", [128, 32, 512], BF16)
                BM = sb("BM", [128, 4, 256], F32)
                rbt = sb("rbt", [32, 12], F32)
                RB = sb("RB", [33, 4, 128], F32)
                oht = sb("oht", [33, 384], F32)
                Ub = sb("Ub", [128, 384], F32)
                ssb = [sb("ss%d" % i, [128, 256], F32) for i in range(2)]
                pb = [sb("pb%d" % i, [128, 256], BF16) for i in range(2)]
                ptb = [sb("pt%d" % i, [128, 256], BF16) for i in range(2)]
                ol = [sb("ol%d" % i, [128, 4, 129], F32) for i in range(2)]
                stt = [sb("st%d" % i, [128, 8], F32) for i in range(4)]
                P.op("sp", lambda e: e.dma_start(out=rbt[:], in_=relb_in), writes=["rbt"], dma=True)
                it = 0
                bi = 0
                for g, (win, dil) in enumerate(GROUPS):
                    nb = 32 // dil
                    P.op("sp", lambda e, g=g: e.dma_start(out=oht[:], in_=ohu_in[g]), writes=["oht"], dma=True)
                    P.op("pool", lambda e: e.memset(RB[32:33, :, :], 1.0), writes=["RB1"])
                    for j in range(4):
                        P.op("dve", lambda e, j=j, g=g: e.tensor_scalar(out=RB[0:32, j, :], in0=ones[0:32, :], scalar1=rbt[0:32, g * 4 + j:g * 4 + j + 1], scalar2=None, op0=ALU.mult),
                             reads=["rbt", "ones"], writes=[("RB", j)])
                        P.op("pe", lambda e, j=j: e.matmul(psU[:, 0:384], lhsT=RB[0:33, j, :], rhs=oht[0:33, :], start=True, stop=True),
                             reads=[("RB", j), "RB1", "oht"], writes=["psU"])
                        P.op("dve", lambda e: e.tensor_copy(out=Ub[:], in_=psU[:, 0:384]), reads=["psU"], writes=["Ub"])
                        hd = g * 4 + j
                        P.op("sp", lambda e, hd=hd: e.dma_start(out=toep_d[hd].rearrange("(p i) -> p i", p=128), in_=Ub[:]), reads=["Ub"], writes=[("toep", hd)], dma=True)
                        P.op("sp", lambda e, hd=hd, j=j: e.dma_start(out=BM[:, j, :], in_=bass.AP(toep_t, hd * 128 * 384 + 127, [[383, 128], [1, 256]])),
                             reads=[("toep", hd)], writes=[("BM", j)], dma=True)
                    for j in range(4):
                        hd = g * 4 + j
                        P.op("sp", lambda e, j=j, hd=hd: e.dma_start(out=QK[j][:], in_=qT_d[hd * 128:(hd + 1) * 128, :]), writes=[("Q", j)], dma=True)
                        P.op("sp", lambda e, j=j, hd=hd: e.dma_start(out=QK[4 + j][:], in_=kT_d[hd * 128:(hd + 1) * 128, :]), writes=[("K", j)], dma=True)
                    vsrc = v_d[:, g * 512:(g + 1) * 512].rearrange("(b p r) c -> p b r c", p=128, r=dil)
                    Vv = Vg[:].rearrange("p (b r) c -> p b r c", r=dil)
                    for r in range(dil):
                        P.op("sp", lambda e, r=r, vsrc=vsrc, Vv=Vv: e.dma_start(out=Vv[:, :, r, :], in_=vsrc[:, :, r, :]), writes=[("V", r)], dma=True)
                    for r in range(dil):
                        for b in range(nb):
                            olb = ol[bi % 2]
                            bi += 1
                            for j in range(4):
                                q0 = b * 128 * dil + r
                                if b == 0:
                                    nk, k0, koff = 128, r, 128
                                else:
                                    nk, k0, koff = 256, (b - 1) * 128 * dil + r, 0
                                qsl = QK[j][:, q0:q0 + 127 * dil + 1:dil]
                                ksl = QK[4 + j][:, k0:k0 + (nk - 1) * dil + 1:dil]
                                pS, sS, pB, pT, pTb, pO, sa = psS[it % 2], ssb[it % 2], pb[it % 2], psT[it % 2], ptb[it % 2], psO[it % 2], stt[it % 4]
                                it += 1
                                P.op("pe", lambda e, pS=pS, qsl=qsl, ksl=ksl, nk=nk: e.matmul(pS[:, 0:nk], lhsT=qsl, rhs=ksl, start=True, stop=True),
                                     reads=[("Q", j), ("K", j)], writes=[pS.name])
                                P.op("dve", lambda e, pS=pS, sS=sS, nk=nk, j=j, koff=koff: e.scalar_tensor_tensor(
                                    out=sS[:, 0:nk], in0=pS[:, 0:nk], scalar=1.0 / math.sqrt(128.0), in1=BM[:, j, koff:koff + nk], op0=ALU.mult, op1=ALU.add),
                                    reads=[pS.name, ("BM", j)], writes=[sS.name])
                                P.op("dve", lambda e, sS=sS, sa=sa, nk=nk: e.reduce_max(out=sa[:, 0:1], in_=sS[:, 0:nk], axis=AX.X), reads=[sS.name], writes=[(sa.name, 0)])
                                P.op("dve", lambda e, sa=sa: e.tensor_scalar(out=sa[:, 1:2], in0=sa[:, 0:1], scalar1=-1.0, scalar2=None, op0=ALU.mult),
                                     reads=[(sa.name, 0)], writes=[(sa.name, 1)])
                                P.op("act", lambda e, sS=sS, pB=pB, sa=sa, nk=nk: e.activation(out=pB[:, 0:nk], in_=sS[:, 0:nk], func=AF.Exp, bias=sa[:, 1:2], scale=1.0, accum_out=sa[:, 2:3]),
                                     reads=[sS.name, (sa.name, 1)], writes=[pB.name, (sa.name, 2)])
                                P.op("pe", [lambda e, pT=pT, pB=pB, i=i: e.transpose(pT[:, i * 128:(i + 1) * 128], pB[:, i * 128:(i + 1) * 128], ident[:]) for i in range(nk // 128)],
                                     reads=[pB.name, "ident"], writes=[pT.name])
                                P.op("act", lambda e, pT=pT, pTb=pTb, nk=nk: e.activation(out=pTb[:, 0:nk], in_=pT[:, 0:nk], func=AF.Copy), reads=[pT.name], writes=[pTb.name])
                                vb0 = b if b == 0 else b - 1
                                P.op("pe", [lambda e, pO=pO, pTb=pTb, i=i, vb0=vb0, r=r, j=j, nk=nk, Vv=Vv: e.matmul(
                                    pO[:], lhsT=pTb[:, i * 128:(i + 1) * 128], rhs=Vv[:, vb0 + i, r, j * 128:(j + 1) * 128], start=(i == 0), stop=(i == nk // 128 - 1))
                                    for i in range(nk // 128)], reads=[pTb.name, ("V", r)], writes=[pO.name])
                                P.op("dve", lambda e, sa=sa: e.reciprocal(out=sa[:, 3:4], in_=sa[:, 2:3]), reads=[(sa.name, 2)], writes=[(sa.name, 3)])
                                P.op("dve", lambda e, pO=pO, olb=olb, j=j, sa=sa: e.tensor_scalar(out=olb[:, j, 0:128], in0=pO[:], scalar1=sa[:, 3:4], scalar2=None, op0=ALU.mult),
                                     reads=[pO.name, (sa.name, 3)], writes=[(olb.name, j)])
                                P.op("act", lambda e, sa=sa: e.activation(out=sa[:, 4:5], in_=sa[:, 2:3], func=AF.Ln), reads=[(sa.name, 2)], writes=[(sa.name, 4)])
                                P.op("pool", lambda e, sa=sa, olb=olb, j=j: e.tensor_tensor(out=olb[:, j, 128:129], in0=sa[:, 4:5], in1=sa[:, 0:1], op=ALU.add),
                                     reads=[(sa.name, 4), (sa.name, 0)], writes=[(olb.name, ("l", j))])
                            dst = ao_d[g].rearrange("(b p r) c -> p b r c", p=128, r=dil)[:, b, r, :]
                            P.op("sp", lambda e, dst=dst, olb=olb: e.dma_start(out=dst, in_=olb[:].rearrange("p j c -> p (j c)")),
                                 reads=[(olb.name, j) for j in range(4)] + [(olb.name, ("l", j)) for j in range(4)], dma=True)
                P.flush()

        def stage_C(l):
            if run("C1") or run("C"):
                stage_C1(l)
            if run("C23") or run("C"):
                stage_C23(l)

        def stage_C1(l):
            with ExitStack() as st:
                def sb(name, shape, dt):
                    return st.enter_context(SBT("C1_" + name, list(shape), dt))
                u4 = [sb("u%d" % i, [128, 4, S + 3], BF16) for i in range(2)]
                acc = [sb("acc%d" % i, [128, S], F32) for i in range(2)]
                cv4 = [sb("cv%d" % i, [128, 4, S], BF16) for i in range(2)]
                tst = [sb("tst%d" % i, [128, 8, 512], BF16) for i in range(2)]
                cw = sb("cw", [128, 48, 4], F32)
                cbias = sb("cb", [128, 48], F32)
                psT = [st.enter_context(PST("C1p%d" % i, [128, 512], BF16)) for i in range(4)]
                P.op("sp", lambda e: e.dma_start(out=cw[:], in_=convw_in[l]), writes=["cw"], dma=True)
                P.op("sp", lambda e: e.dma_start(out=cbias[:], in_=convb_in[l]), writes=["cbias"], dma=True)
                for i in range(2):
                    P.op("pool", lambda e, i=i: e.memset(u4[i][:, :, 0:3], 0.0), writes=[(u4[i].name, "pad")])
                ti = 0
                pi = 0
                ai = 0
                for G in range(12):
                    ub, cvb = u4[G % 2], cv4[G % 2]
                    P.op("sp", lambda e, ub=ub, G=G: e.dma_start(out=ub[:, :, 3:3 + S], in_=xbcT_d[G * 512:(G + 1) * 512, :].rearrange("(k p) t -> p k t", p=128)),
                         writes=[(ub.name, "d")], dma=True)
                    for k in range(4):
                        cb = G * 4 + k
                        ac = acc[ai % 2]
                        ai += 1
                        P.op("dve", lambda e, ac=ac, ub=ub, k=k, cb=cb: e.tensor_scalar(out=ac[:], in0=ub[:, k, 0:S], scalar1=cw[:, cb, 0:1], scalar2=None, op0=ALU.mult),
                             reads=[(ub.name, "d"), (ub.name, "pad"), "cw"], writes=[ac.name])
                        for jj in range(1, 4):
                            eng = "dve"
                            P.op(eng, lambda e, ac=ac, ub=ub, k=k, cb=cb, jj=jj: e.scalar_tensor_tensor(
                                out=ac[:], in0=ub[:, k, jj:jj + S], scalar=cw[:, cb, jj:jj + 1], in1=ac[:], op0=ALU.mult, op1=ALU.add),
                                reads=[(ub.name, "d"), (ub.name, "pad"), "cw", ac.name], writes=[ac.name])
                        P.op("act", lambda e, ac=ac, cvb=cvb, k=k, cb=cb: e.activation(out=cvb[:, k, :], in_=ac[:], func=AF.Silu, bias=cbias[:, cb:cb + 1], scale=1.0),
                             reads=[ac.name, "cbias"], writes=[(cvb.name, k)])
                    if G < 10:
                        dest, coff = (xst_d, G * 512) if G < 8 else (bt_d, (G - 8) * 512)
                        for c8 in range(4):
                            ts_ = tst[ti % 2]
                            ti += 1
                            for q in range(8):
                                c = c8 * 8 + q
                                pt = psT[pi % 4]
                                pi += 1
                                P.op("pe", [lambda e, pt=pt, cvb=cvb, k=k, c=c: e.transpose(pt[:, k * 128:(k + 1) * 128], cvb[:, k, c * 128:(c + 1) * 128], ident[:]) for k in range(4)],
                                     reads=[(cvb.name, k) for k in range(4)] + ["ident"], writes=[pt.name])
                                if q % 2 == 0:
                                    P.op("act", lambda e, pt=pt, ts_=ts_, q=q: e.activation(out=ts_[:, q, :], in_=pt[:], func=AF.Copy), reads=[pt.name], writes=[(ts_.name, q)])
                                else:
                                    P.op("dve", lambda e, pt=pt, ts_=ts_, q=q: e.tensor_copy(out=ts_[:, q, :], in_=pt[:]), reads=[pt.name], writes=[(ts_.name, q)])
                            P.op("sp", lambda e, ts_=ts_, dest=dest, coff=coff, c8=c8: e.dma_start(
                                out=dest[c8 * 1024:(c8 + 1) * 1024, coff:coff + 512].rearrange("(q p) c -> p q c", p=128), in_=ts_[:]),
                                reads=[(ts_.name, q) for q in range(8)], dma=True)
                    if G >= 8:
                        dest, r0 = (BT_d, (G - 8) * 512) if G < 10 else (CT_d, (G - 10) * 512)
                        P.op("sp", lambda e, cvb=cvb, dest=dest, r0=r0: e.dma_start(out=dest[r0:r0 + 512, :].rearrange("(k p) t -> p k t", p=128), in_=cvb[:]),
                             reads=[(cvb.name, k) for k in range(4)], dma=True)
                P.flush()
        def stage_C23(l):
            with ExitStack() as st:
                def sb(name, shape, dt):
                    return st.enter_context(SBT("C3_" + name, list(shape), dt))
                dt_sb = sb("dt", [128, 32, 64], F32)
                dtA = sb("dtA", [128, 32, 64], F32)
                expA = sb("expA", [128, 32, 64], F32)
                cdec = sb("cdec", [128, 32, 64], F32)
                dtdec = sb("dtdec", [128, 32, 64], F32)
                Ab = sb("Ab", [128, 64], F32)
                dtb = sb("dtb", [128, 64], F32)
                dsk = sb("dsk", [128, 4096], F32)
                snw = sb("snw", [128, 32], F32)
                hst = sb("hst", [128, 8, 512], F32)
                hbf = sb("hbf", [128, 8, 512], BF16)
                with ExitStack() as s2:
                    print("C2 start nops", P.nops)
                    psA = s2.enter_context(PST("C2pA", [128, 2048], F32))
                    psL = s2.enter_context(PST("C2pL", [128, 2048], F32))
                    P.op("sp", lambda e: e.dma_start(out=dt_sb[:], in_=dtr_d.rearrange("(c p) h -> p c h", p=128)), writes=["dt"], dma=True)
                    P.op("sp", lambda e: e.dma_start(out=Ab[:], in_=alog_in[l]), writes=["Ab"], dma=True)
                    P.op("sp", lambda e: e.dma_start(out=dtb[:], in_=dtb_in[l]), writes=["dtb"], dma=True)
                    P.op("sp", lambda e: e.dma_start(out=dsk[:], in_=dsk_in[l]), writes=["dsk"], dma=True)
                    P.op("sp", lambda e: e.dma_start(out=snw[:], in_=snw_in[l]), writes=["snw"], dma=True)
                    bc = lambda t: t[:].unsqueeze(1).to_broadcast([128, 32, 64])
                    P.op("dve", lambda e: e.tensor_tensor(out=dt_sb[:], in0=dt_sb[:], in1=bc(dtb), op=ALU.add), reads=["dt", "dtb"], writes=["dt"])
                    P.op("act", lambda e: e.activation(out=expA[:], in_=dt_sb[:], func=AF.Abs), reads=["dt"], writes=["expA"])
                    P.op("act", lambda e: e.activation(out=expA[:], in_=expA[:], func=AF.Exp, scale=-1.0), reads=["expA"], writes=["expA"])
                    P.op("act", lambda e: e.activation(out=expA[:], in_=expA[:], func=AF.Ln, bias=1.0, scale=1.0), reads=["expA"], writes=["expA"])
                    P.op("dve", lambda e: e.tensor_scalar_max(out=dt_sb[:], in0=dt_sb[:], scalar1=0.0), reads=["dt"], writes=["dt"])
                    P.op("dve", lambda e: e.tensor_tensor(out=dt_sb[:], in0=dt_sb[:], in1=expA[:], op=ALU.add), reads=["dt", "expA"], writes=["dt"])
                    P.op("act", lambda e: e.activation(out=Ab[:], in_=Ab[:], func=AF.Exp), reads=["Ab"], writes=["Ab"])
                    P.op("dve", lambda e: e.tensor_scalar(out=Ab[:], in0=Ab[:], scalar1=-1.0, scalar2=None, op0=ALU.mult), reads=["Ab"], writes=["Ab"])
                    P.op("dve", lambda e: e.tensor_tensor(out=dtA[:], in0=dt_sb[:], in1=bc(Ab), op=ALU.mult), reads=["dt", "Ab"], writes=["dtA"])
                    P.op("pe", [lambda e, c=c: e.matmul(psA[:, c * 64:(c + 1) * 64], lhsT=tri[:], rhs=dtA[:, c, :], start=True, stop=True) for c in range(32)],
                         reads=["dtA", "tri"], writes=["psA"])
                    P.op("pe", [lambda e, c=c: e.matmul(psL[:, c * 64:(c + 1) * 64], lhsT=ones[:], rhs=dtA[:, c, :], start=True, stop=True) for c in range(32)],
                         reads=["dtA", "ones"], writes=["psL"])
                    fl = lambda t: t[:].rearrange("p c h -> p (c h)")
                    for bk_ in range(4):
                        bs = slice(bk_ * 512, (bk_ + 1) * 512)
                        P.op("dve", lambda e, bs=bs: e.tensor_copy(out=fl(dtdec)[:, bs], in_=psA[:, bs]), reads=[], writes=[("dtdec", bk_), "psA", "psL"])
                        P.op("act", lambda e, bs=bs: e.activation(out=fl(expA)[:, bs], in_=psA[:, bs], func=AF.Exp), reads=["dt", "expA"], writes=[("expA", bk_), "psA", "psL"])
                        P.op("act", lambda e, bs=bs: e.activation(out=fl(cdec)[:, bs], in_=psL[:, bs], func=AF.Exp), reads=[], writes=[("cdec", bk_), "psA", "psL"])
                        P.op("dve", lambda e, bs=bs: e.tensor_tensor(out=fl(dtdec)[:, bs], in0=psL[:, bs], in1=fl(dtdec)[:, bs], op=ALU.subtract), reads=[("dtdec", bk_)], writes=[("dtdec", bk_), "psA", "psL"])
                    P.op("act", lambda e: e.activation(out=fl(dtdec), in_=fl(dtdec), func=AF.Exp), reads=[("dtdec", b_) for b_ in range(4)], writes=["dtdec"])
                    P.op("dve", lambda e: e.tensor_tensor(out=dtdec[:], in0=dtdec[:], in1=dt_sb[:], op=ALU.mult), reads=["dtdec", "dt"], writes=["dtdec"])
                    P.op("dve", lambda e: e.memset(hst[:], 0.0), writes=["hst"])
                    P.op("pool", lambda e: e.memset(hbf[:], 0.0), writes=["hbf"])
                    P.flush()
                print("C2 end nops", P.nops)
                if only is not None and "noC3" in only:
                    return
                BTs = [sb("BTs%d" % i, [128, 8, 512], BF16) for i in range(2)]
                CTs = [sb("CTs%d" % i, [128, 8, 512], BF16) for i in range(2)]
                xst = [sb("xst%d" % i, [128, 4096], BF16) for i in range(2)]
                btk = [sb("btk%d" % i, [128, 1024], BF16) for i in range(2)]
                zt = [sb("zt%d" % i, [128, 512], BF16) for i in range(2)]
                X = [sb("X%d" % i, [128, 8, 128], F32) for i in range(2)]
                Ld = [sb("Ld%d" % i, [128, 8, 128], F32) for i in range(2)]
                MT = [sb("MT%d" % i, [128, 8, 128], BF16) for i in range(2)]
                cbm = [sb("cbm%d" % i, [128, 128], F32) for i in range(2)]
                xdt = [sb("xdt%d" % i, [128, 512], BF16) for i in range(2)]
                xdd = [sb("xdd%d" % i, [128, 512], BF16) for i in range(2)]
                t1 = [sb("t1%d" % i, [128, 512], F32) for i in range(2)]
                t2 = [sb("t2%d" % i, [128, 512], F32) for i in range(2)]
                t3 = [sb("t3%d" % i, [128, 512], F32) for i in range(2)]
                szt = [sb("sz%d" % i, [128, 512], F32) for i in range(2)]
                junk = sb("junk", [128, 512], F32)
                yn = [sb("yn%d" % i, [128, 512], BF16) for i in range(2)]
                yts = [sb("yts%d" % i, [128, 4, 128], BF16) for i in range(2)]
                sst = [sb("sst%d" % i, [128, 4], F32) for i in range(4)]
                psD = st.enter_context(PST("C3pD", [128, 1024], F32))
                psY = st.enter_context(PST("C3pY", [128, 512], F32))
                psYo = st.enter_context(PST("C3pYo", [128, 512], F32))
                psSt = st.enter_context(PST("C3pS", [128, 512], F32))
                psTr = st.enter_context(PST("C3pT", [128, 512], BF16))
                psCb = st.enter_context(PST("C3pC", [128, 128], F32))
                it = 0
                for sc in range(8):
                    Bs, Cs = BTs[sc % 2], CTs[sc % 2]
                    P.op("sp", lambda e, Bs=Bs, sc=sc: e.dma_start(out=Bs[:], in_=BT_d[:, sc * 512:(sc + 1) * 512].rearrange("(g n) t -> n g t", n=128)), writes=[Bs.name], dma=True)
                    P.op("sp", lambda e, Cs=Cs, sc=sc: e.dma_start(out=Cs[:], in_=CT_d[:, sc * 512:(sc + 1) * 512].rearrange("(g n) t -> n g t", n=128)), writes=[Cs.name], dma=True)
                    for cc in range(4):
                        c = sc * 4 + cc
                        xs_, bk = xst[c % 2], btk[c % 2]
                        P.op("sp", lambda e, xs_=xs_, c=c: e.dma_start(out=xs_[:], in_=xst_d[c * 128:(c + 1) * 128, :]), writes=[xs_.name], dma=True)
                        P.op("sp", lambda e, bk=bk, c=c: e.dma_start(out=bk[:], in_=bt_d[c * 128:(c + 1) * 128, :]), writes=[bk.name], dma=True)
                        for g in range(8):
                            k = it % 2
                            it += 1
                            z_ = zt[k]
                            P.op("sp", lambda e, z_=z_, c=c, g=g: e.dma_start(out=z_[:], in_=z_d[c * 128:(c + 1) * 128, g * 512:(g + 1) * 512]), writes=[z_.name], dma=True)
                            tsl = slice(cc * 128, (cc + 1) * 128)
                            hs = slice(g * 8, (g + 1) * 8)
                            P.op("pe", lambda e, Bs=Bs, Cs=Cs, g=g, tsl=tsl: e.matmul(psCb[:], lhsT=Bs[:, g, tsl], rhs=Cs[:, g, tsl], start=True, stop=True),
                                 reads=[Bs.name, Cs.name], writes=["psCb"])
                            P.op("dve", lambda e, k=k: e.tensor_tensor(out=cbm[k][:], in0=psCb[:], in1=tri[:], op=ALU.mult), reads=["psCb", "tri"], writes=[cbm[k].name])
                            P.op("pool", lambda e, k=k, c=c, hs=hs: e.tensor_tensor(out=X[k][:], in0=tri[:].unsqueeze(1).to_broadcast([128, 8, 128]),
                                                                                   in1=dtA[:, c, hs].unsqueeze(2).to_broadcast([128, 8, 128]), op=ALU.mult),
                                 reads=["tri", "dtA"], writes=[X[k].name])
                            P.op("pe", [lambda e, k=k, h2=h2: e.matmul(psD[:, h2 * 512:(h2 + 1) * 512], lhsT=gst[:], rhs=X[k][:, h2 * 4:(h2 + 1) * 4, :].rearrange("p r l -> p (r l)"), start=True, stop=True)
                                        for h2 in range(2)], reads=[X[k].name, "gst"], writes=["psD"])
                            for h2 in range(2):
                                P.op("act", lambda e, k=k, h2=h2: e.activation(out=Ld[k][:, h2 * 4:(h2 + 1) * 4, :].rearrange("p r l -> p (r l)"), in_=psD[:, h2 * 512:(h2 + 1) * 512], func=AF.Exp),
                                     reads=["psD"], writes=[(Ld[k].name, h2)])
                            P.op("dve", lambda e, k=k: e.tensor_tensor(out=MT[k][:], in0=Ld[k][:], in1=cbm[k][:].unsqueeze(1).to_broadcast([128, 8, 128]), op=ALU.mult),
                                 reads=[(Ld[k].name, 0), (Ld[k].name, 1), cbm[k].name], writes=[MT[k].name])
                            xg = xs_[:, g * 512:(g + 1) * 512].rearrange("p (r q) -> p r q", q=64)
                            P.op("pool", lambda e, k=k, xg=xg, c=c, hs=hs: e.tensor_tensor(out=xdt[k][:].rearrange("p (r q) -> p r q", q=64), in0=xg,
                                                                                          in1=dt_sb[:, c, hs].unsqueeze(2).to_broadcast([128, 8, 64]), op=ALU.mult),
                                 reads=[xs_.name, "dt"], writes=[xdt[k].name])
                            P.op("pool", lambda e, k=k, xg=xg, c=c, hs=hs: e.tensor_tensor(out=xdd[k][:].rearrange("p (r q) -> p r q", q=64), in0=xg,
                                                                                          in1=dtdec[:, c, hs].unsqueeze(2).to_broadcast([128, 8, 64]), op=ALU.mult),
                                 reads=[xs_.name, "dtdec"], writes=[xdd[k].name])
                            P.op("pe", [lambda e, k=k, r=r: e.matmul(psY[:, r * 64:(r + 1) * 64], lhsT=MT[k][:, r, :], rhs=xdt[k][:, r * 64:(r + 1) * 64], start=True, stop=True) for r in range(8)],
                                 reads=[MT[k].name, xdt[k].name], writes=["psY"])
                            P.op("pe", lambda e, Cs=Cs, g=g, tsl=tsl: e.matmul(psYo[:], lhsT=Cs[:, g, tsl], rhs=hbf[:, g, :], start=True, stop=True),
                                 reads=[Cs.name, ("hbf", g)], writes=["psYo"])
                            P.op("pe", lambda e, bk=bk, g=g, k=k: e.matmul(psSt[:], lhsT=bk[:, g * 128:(g + 1) * 128], rhs=xdd[k][:], start=True, stop=True),
                                 reads=[bk.name, xdd[k].name], writes=["psSt"])
                            b8 = lambda ap: ap.unsqueeze(2).to_broadcast([128, 8, 64])
                            v3 = lambda ap: ap.rearrange("p (r q) -> p r q", q=64)
                            P.op("dve", lambda e, k=k, c=c, hs=hs: e.tensor_tensor(out=v3(t1[k][:]), in0=v3(psYo[:]), in1=b8(expA[:, c, hs]), op=ALU.mult),
                                 reads=["psYo", "expA"], writes=[t1[k].name])
                            P.op("dve", lambda e, k=k: e.tensor_tensor(out=t2[k][:], in0=psY[:], in1=t1[k][:], op=ALU.add), reads=["psY", t1[k].name], writes=[t2[k].name])
                            P.op("pool", lambda e, k=k, xs_=xs_, g=g: e.tensor_tensor(out=t3[k][:], in0=xs_[:, g * 512:(g + 1) * 512], in1=dsk[:, g * 512:(g + 1) * 512], op=ALU.mult),
                                 reads=[xs_.name, "dsk"], writes=[t3[k].name])
                            P.op("pool", lambda e, k=k: e.tensor_tensor(out=t3[k][:], in0=t3[k][:], in1=t2[k][:], op=ALU.add), reads=[t3[k].name, t2[k].name], writes=[t3[k].name])
                            P.op("act", lambda e, k=k, z_=z_: e.activation(out=szt[k][:], in_=z_[:], func=AF.Silu), reads=[z_.name], writes=[szt[k].name])
                            P.op("dve", lambda e, k=k: e.tensor_tensor(out=t2[k][:], in0=t3[k][:], in1=szt[k][:], op=ALU.mult), reads=[t3[k].name, szt[k].name], writes=[t2[k].name])
                            sa = sst[it % 4]
                            P.op("act", lambda e, k=k, sa=sa: e.activation(out=junk[:], in_=t2[k][:], func=AF.Square, accum_out=sa[:, 0:1]), reads=[t2[k].name], writes=["junk", (sa.name, 0)])
                            P.op("dve", lambda e, sa=sa: e.tensor_scalar(out=sa[:, 1:2], in0=sa[:, 0:1], scalar1=1.0 / 512.0, scalar2=EPS, op0=ALU.mult, op1=ALU.add),
                                 reads=[(sa.name, 0)], writes=[(sa.name, 1)])
                            P.op("act", lambda e, sa=sa: e.activation(out=sa[:, 2:3], in_=sa[:, 1:2], func=AF.Sqrt), reads=[(sa.name, 1)], writes=[(sa.name, 2)])
                            P.op("dve", lambda e, sa=sa: e.reciprocal(out=sa[:, 3:4], in_=sa[:, 2:3]), reads=[(sa.name, 2)], writes=[(sa.name, 3)])
                            P.op("dve", lambda e, k=k, sa=sa: e.tensor_scalar(out=yn[k][:], in0=t2[k][:], scalar1=sa[:, 3:4], scalar2=None, op0=ALU.mult),
                                 reads=[t2[k].name, (sa.name, 3)], writes=[yn[k].name])
                            P.op("pe", [lambda e, k=k, q=q: e.transpose(psTr[:, q * 128:(q + 1) * 128], yn[k][:, q * 128:(q + 1) * 128], ident[:]) for q in range(4)],
                                 reads=[yn[k].name, "ident"], writes=["psTr"])
                            P.op("dve", lambda e, k=k, g=g: e.tensor_tensor(out=yts[k][:], in0=psTr[:].rearrange("p (q t) -> p q t", t=128),
                                                                           in1=snw[:, g * 4:(g + 1) * 4].unsqueeze(2).to_broadcast([128, 4, 128]), op=ALU.mult),
                                 reads=["psTr", "snw"], writes=[yts[k].name])
                            P.op("sp", lambda e, k=k, g=g, c=c: e.dma_start(out=yT_d[g * 512:(g + 1) * 512, c * 128:(c + 1) * 128].rearrange("(q p) t -> p q t", p=128), in_=yts[k][:]),
                                 reads=[yts[k].name], dma=True)
                            P.op("dve", lambda e, g=g, c=c, hs=hs: e.tensor_tensor(out=v3(hst[:, g, :]), in0=v3(hst[:, g, :]), in1=b8(cdec[:, c, hs]), op=ALU.mult),
                                 reads=[("hst", g), "cdec"], writes=[("hst", g)])
                            P.op("dve", lambda e, g=g: e.tensor_tensor(out=hst[:, g, :], in0=hst[:, g, :], in1=psSt[:], op=ALU.add), reads=[("hst", g), "psSt"], writes=[("hst", g)])
                            P.op("act", lambda e, g=g: e.activation(out=hbf[:, g, :], in_=hst[:, g, :], func=AF.Copy), reads=[("hst", g)], writes=[("hbf", g)])
                P.flush()

        def stage_D(l, src_d):
            with ExitStack() as st:
                def sb(name, shape, dt):
                    return st.enter_context(SBT("D_" + name, list(shape), dt))
                yTt = sb("yT", [128, 32, 512], BF16)
                atT = sb("atT", [128, 4, 512], BF16)
                mgT = sb("mgT", [128, 16, 512], BF16)
                wa = [sb("wa%d" % i, [128, 4, 256], BF16) for i in range(2)]
                ws = [sb("ws%d" % i, [128, 32, 256], BF16) for i in range(2)]
                wo = [sb("wo%d" % i, [128, 16, 256], BF16) for i in range(2)]
                ao = [sb("ao%d" % i, [128, 3, 4, 129], F32) for i in range(2)]
                E3 = [sb("E3%d" % i, [128, 3, 4], F32) for i in range(2)]
                Mx = [sb("Mx%d" % i, [128, 4], F32) for i in range(2)]
                Dn = [sb("Dn%d" % i, [128, 4], F32) for i in range(2)]
                acc = [sb("acc%d" % i, [128, 4, 128], F32) for i in range(2)]
                tmp = [sb("tmp%d" % i, [128, 4, 128], F32) for i in range(2)]
                atk = [sb("atk%d" % i, [128, 512], BF16) for i in range(2)]
                gat = [sb("ga%d" % i, [128, 512], BF16) for i in range(2)]
                gst_ = [sb("gs%d" % i, [128, 512], BF16) for i in range(2)]
                sga = [sb("sga%d" % i, [128, 512], F32) for i in range(2)]
                sgs = [sb("sgs%d" % i, [128, 512], F32) for i in range(2)]
                m1 = [sb("m1%d" % i, [128, 512], F32) for i in range(2)]
                m2 = [sb("m2%d" % i, [128, 512], F32) for i in range(2)]
                xt = [sb("xt%d" % i, [128, 512], F32) for i in range(2)]
                ps1 = [st.enter_context(PST("Dp1%d" % i, [128, 512], F32)) for i in range(2)]
                ps2 = [st.enter_context(PST("Dp2%d" % i, [128, 512], F32)) for i in range(2)]
                ps3 = [st.enter_context(PST("Dp3%d" % i, [128, 512], F32)) for i in range(2)]
                psT = st.enter_context(PST("DpT", [128, 512], BF16))
                wi = 0
                ei = 0
                ab = 0
                for tt in range(8):
                    tsl = slice(tt * 512, (tt + 1) * 512)
                    for hh in range(4):
                        P.op("sp", lambda e, hh=hh, tsl=tsl: e.dma_start(out=yTt[:, hh * 8:(hh + 1) * 8, :], in_=yT_d[hh * 1024:(hh + 1) * 1024, tsl].rearrange("(c p) t -> p c t", p=128)),
                             writes=[("yT", hh)], dma=True)
                    for q4 in range(4):
                        tb = tt * 4 + q4
                        a_, e3, mx, dn, ac, tm, ak = ao[ab % 2], E3[ab % 2], Mx[ab % 2], Dn[ab % 2], acc[ab % 2], tmp[ab % 2], atk[ab % 2]
                        ab += 1
                        P.op("sp", lambda e, a_=a_, tb=tb: e.dma_start(out=a_[:].rearrange("p g j c -> p g (j c)"), in_=ao_d[:, tb * 128:(tb + 1) * 128, :].rearrange("g p c -> p g c")),
                             writes=[a_.name], dma=True)
                        P.op("dve", lambda e, a_=a_, mx=mx: e.tensor_tensor(out=mx[:], in0=a_[:, 0, :, 128], in1=a_[:, 1, :, 128], op=ALU.max), reads=[a_.name], writes=[mx.name])
                        P.op("dve", lambda e, a_=a_, mx=mx: e.tensor_tensor(out=mx[:], in0=mx[:], in1=a_[:, 2, :, 128], op=ALU.max), reads=[a_.name, mx.name], writes=[mx.name])
                        P.op("dve", lambda e, a_=a_, mx=mx, e3=e3: e.tensor_tensor(out=e3[:], in0=a_[:, :, :, 128], in1=mx[:].unsqueeze(1).to_broadcast([128, 3, 4]), op=ALU.subtract),
                             reads=[a_.name, mx.name], writes=[e3.name])
                        P.op("act", lambda e, e3=e3: e.activation(out=e3[:], in_=e3[:], func=AF.Exp), reads=[e3.name], writes=[e3.name])
                        P.op("dve", lambda e, e3=e3, dn=dn: e.tensor_tensor(out=dn[:], in0=e3[:, 0, :], in1=e3[:, 1, :], op=ALU.add), reads=[e3.name], writes=[dn.name])
                        P.op("dve", lambda e, e3=e3, dn=dn: e.tensor_tensor(out=dn[:], in0=dn[:], in1=e3[:, 2, :], op=ALU.add), reads=[e3.name, dn.name], writes=[dn.name])
                        P.op("dve", lambda e, dn=dn: e.reciprocal(out=dn[:], in_=dn[:]), reads=[dn.name], writes=[dn.name])
                        P.op("dve", lambda e, e3=e3, dn=dn: e.tensor_tensor(out=e3[:], in0=e3[:], in1=dn[:].unsqueeze(1).to_broadcast([128, 3, 4]), op=ALU.mult),
                             reads=[e3.name, dn.name], writes=[e3.name])
                        bw = lambda e3, g: e3[:, g, :].unsqueeze(2).to_broadcast([128, 4, 128])
                        P.op("dve", lambda e, a_=a_, e3=e3, ac=ac: e.tensor_tensor(out=ac[:], in0=a_[:, 0, :, 0:128], in1=bw(e3, 0), op=ALU.mult), reads=[a_.name, e3.name], writes=[ac.name])
                        for g in (1, 2):
                            P.op("pool", lambda e, a_=a_, e3=e3, tm=tm, g=g: e.tensor_tensor(out=tm[:], in0=a_[:, g, :, 0:128], in1=bw(e3, g), op=ALU.mult),
                                 reads=[a_.name, e3.name], writes=[tm.name])
                            if g == 1:
                                P.op("dve", lambda e, ac=ac, tm=tm: e.tensor_tensor(out=ac[:], in0=ac[:], in1=tm[:], op=ALU.add), reads=[ac.name, tm.name], writes=[ac.name])
                            else:
                                P.op("dve", lambda e, ac=ac, tm=tm, ak=ak: e.tensor_tensor(out=ak[:].rearrange("p (j c) -> p j c", c=128), in0=ac[:], in1=tm[:], op=ALU.add),
                                     reads=[ac.name, tm.name], writes=[ak.name])
                        P.op("pe", [lambda e, ak=ak, j=j: e.transpose(psT[:, j * 128:(j + 1) * 128], ak[:, j * 128:(j + 1) * 128], ident[:]) for j in range(4)],
                             reads=[ak.name, "ident"], writes=["psT"])
                        P.op("act", lambda e, q4=q4: e.activation(out=atT[:, :, q4 * 128:(q4 + 1) * 128], in_=psT[:].rearrange("p (j t) -> p j t", t=128), func=AF.Copy),
                             reads=["psT"], writes=[("atT", q4)])
                    for cg in range(8):
                        wa_, ws_ = wa[wi % 2], ws[wi % 2]
                        wi += 1
                        csl = slice(cg * 256, (cg + 1) * 256)
                        P.op("pool", lambda e, wa_=wa_, csl=csl: e.dma_start(out=wa_[:], in_=wattn_in[l, :, csl].rearrange("(c p) n -> p c n", p=128)), writes=[wa_.name], dma=True)
                        for hh in range(2):
                            P.op("pool", lambda e, ws_=ws_, csl=csl, hh=hh: e.dma_start(out=ws_[:, hh * 16:(hh + 1) * 16, :], in_=wssm_in[l, hh * 2048:(hh + 1) * 2048, csl].rearrange("(c p) n -> p c n", p=128)),
                                 writes=[(ws_.name, hh)], dma=True)
                        for blk in range(2):
                            nb_ = cg * 2 + blk
                            k = ei % 2
                            ei += 1
                            P.op("sp", lambda e, k=k, nb_=nb_, tsl=tsl: e.dma_start(out=gat[k][:], in_=gaT_d[nb_ * 128:(nb_ + 1) * 128, tsl]), writes=[gat[k].name], dma=True)
                            P.op("sp", lambda e, k=k, nb_=nb_, tsl=tsl: e.dma_start(out=gst_[k][:], in_=gsT_d[nb_ * 128:(nb_ + 1) * 128, tsl]), writes=[gst_[k].name], dma=True)
                            P.op("pe", [lambda e, k=k, wa_=wa_, blk=blk, kc=kc: e.matmul(ps1[k][:], lhsT=wa_[:, kc, blk * 128:(blk + 1) * 128], rhs=atT[:, kc, :], start=(kc == 0), stop=(kc == 3)) for kc in range(4)],
                                 reads=[wa_.name] + [("atT", q) for q in range(4)], writes=[ps1[k].name])
                            P.op("pe", [lambda e, k=k, ws_=ws_, blk=blk, kc=kc: e.matmul(ps2[k][:], lhsT=ws_[:, kc, blk * 128:(blk + 1) * 128], rhs=yTt[:, kc, :], start=(kc == 0), stop=(kc == 31)) for kc in range(32)],
                                 reads=[(ws_.name, 0), (ws_.name, 1)] + [("yT", q) for q in range(4)], writes=[ps2[k].name])
                            P.op("act", lambda e, k=k: e.activation(out=sga[k][:], in_=gat[k][:], func=AF.Sigmoid), reads=[gat[k].name], writes=[sga[k].name])
                            P.op("act", lambda e, k=k: e.activation(out=sgs[k][:], in_=gst_[k][:], func=AF.Sigmoid), reads=[gst_[k].name], writes=[sgs[k].name])
                            P.op("dve", lambda e, k=k: e.tensor_tensor(out=m1[k][:], in0=ps1[k][:], in1=sga[k][:], op=ALU.mult), reads=[ps1[k].name, sga[k].name], writes=[m1[k].name])
                            P.op("dve", lambda e, k=k: e.tensor_tensor(out=m2[k][:], in0=ps2[k][:], in1=sgs[k][:], op=ALU.mult), reads=[ps2[k].name, sgs[k].name], writes=[m2[k].name])
                            P.op("pool", lambda e, k=k, nb_=nb_: e.tensor_tensor(out=mgT[:, nb_, :], in0=m1[k][:], in1=m2[k][:], op=ALU.add), reads=[m1[k].name, m2[k].name], writes=[("mgT", nb_)])
                    for cg in range(8):
                        wo_ = wo[wi % 2]
                        wi += 1
                        csl = slice(cg * 256, (cg + 1) * 256)
                        P.op("pool", lambda e, wo_=wo_, csl=csl: e.dma_start(out=wo_[:], in_=wout_in[l, :, csl].rearrange("(c p) n -> p c n", p=128)), writes=[wo_.name], dma=True)
                        for blk in range(2):
                            nb_ = cg * 2 + blk
                            k = ei % 2
                            ei += 1
                            P.op("sp", lambda e, k=k, nb_=nb_, tsl=tsl: e.dma_start(out=xt[k][:], in_=src_d[nb_ * 128:(nb_ + 1) * 128, tsl]), writes=[xt[k].name], dma=True)
                            P.op("pe", [lambda e, k=k, wo_=wo_, blk=blk, kc=kc: e.matmul(ps3[k][:], lhsT=wo_[:, kc, blk * 128:(blk + 1) * 128], rhs=mgT[:, kc, :], start=(kc == 0), stop=(kc == 15)) for kc in range(16)],
                                 reads=[wo_.name] + [("mgT", q) for q in range(16)], writes=[ps3[k].name])
                            P.op("dve", lambda e, k=k, nb_=nb_: e.scalar_tensor_tensor(out=xt[k][:], in0=ps3[k][:], scalar=modT[:, l, 32 + nb_:33 + nb_], in1=xt[k][:], op0=ALU.mult, op1=ALU.add),
                                 reads=[ps3[k].name, xt[k].name, "modT"], writes=[xt[k].name])
                            P.op("sp", lambda e, k=k, nb_=nb_, tsl=tsl: e.dma_start(out=xs_d[nb_ * 128:(nb_ + 1) * 128, tsl], in_=xt[k][:]), reads=[xt[k].name], dma=True)
                P.flush()

        def stage_E(l):
            with ExitStack() as st:
                def sb(name, shape, dt):
                    return st.enter_context(SBT("E_" + name, list(shape), dt))
                h2 = sb("h2", [128, 16, 512], BF16)
                actT = sb("actT", [128, 44, 512], BF16)
                wg = [sb("wg%d" % i, [128, 16, 256], BF16) for i in range(2)]
                wu = [sb("wu%d" % i, [128, 16, 256], BF16) for i in range(2)]
                wo2 = [sb("wo%d" % i, [128, 44, 256], BF16) for i in range(2)]
                sg = [sb("sg%d" % i, [128, 512], F32) for i in range(2)]
                xt = [sb("xt%d" % i, [128, 512], F32) for i in range(2)]
                pools = norm_pools(st)
                psg = [st.enter_context(PST("Epg%d" % i, [128, 512], F32)) for i in range(2)]
                psu = [st.enter_context(PST("Epu%d" % i, [128, 512], F32)) for i in range(2)]
                pso = [st.enter_context(PST("Epo%d" % i, [128, 512], F32)) for i in range(2)]
                wi = 0
                ei = 0
                for tt in range(8):
                    tsl = slice(tt * 512, (tt + 1) * 512)
                    norm_tiles(xs_d, a2[:, l, :], modT[:, l, 48:64], lambda i, t0, n, tt=tt: (h2[:, i, t0 - tt * 512:t0 - tt * 512 + n], ("h2", i)), tt * 512, 512, pools, "n2")
                    for fg in range(22):
                        wg_, wu_ = wg[wi % 2], wu[wi % 2]
                        wi += 1
                        P.op("pool", lambda e, wg_=wg_, fg=fg: e.dma_start(out=wg_[:], in_=wffi_in[l, :, fg * 256:(fg + 1) * 256].rearrange("(c p) n -> p c n", p=128)), writes=[wg_.name], dma=True)
                        P.op("pool", lambda e, wu_=wu_, fg=fg: e.dma_start(out=wu_[:], in_=wffi_in[l, :, DFF + fg * 256:DFF + (fg + 1) * 256].rearrange("(c p) n -> p c n", p=128)), writes=[wu_.name], dma=True)
                        for blk in range(2):
                            fb = fg * 2 + blk
                            k = ei % 2
                            ei += 1
                            hres = [("h2", i) for i in range(16)]
                            P.op("pe", [lambda e, k=k, wg_=wg_, blk=blk, kc=kc: e.matmul(psg[k][:], lhsT=wg_[:, kc, blk * 128:(blk + 1) * 128], rhs=h2[:, kc, :], start=(kc == 0), stop=(kc == 15)) for kc in range(16)],
                                 reads=[wg_.name] + hres, writes=[psg[k].name])
                            P.op("pe", [lambda e, k=k, wu_=wu_, blk=blk, kc=kc: e.matmul(psu[k][:], lhsT=wu_[:, kc, blk * 128:(blk + 1) * 128], rhs=h2[:, kc, :], start=(kc == 0), stop=(kc == 15)) for kc in range(16)],
                                 reads=[wu_.name] + hres, writes=[psu[k].name])
                            P.op("act", lambda e, k=k: e.activation(out=sg[k][:], in_=psg[k][:], func=AF.Silu), reads=[psg[k].name], writes=[sg[k].name])
                            P.op("dve", lambda e, k=k, fb=fb: e.tensor_tensor(out=actT[:, fb, :], in0=psu[k][:], in1=sg[k][:], op=ALU.mult), reads=[psu[k].name, sg[k].name], writes=[("actT", fb)])
                    for cg in range(8):
                        wo_ = wo2[wi % 2]
                        wi += 1
                        csl = slice(cg * 256, (cg + 1) * 256)
                        for hh in range(4):
                            P.op("pool", lambda e, wo_=wo_, csl=csl, hh=hh: e.dma_start(out=wo_[:, hh * 11:(hh + 1) * 11, :], in_=wffo_in[l, hh * 1408:(hh + 1) * 1408, csl].rearrange("(c p) n -> p c n", p=128)),
                                 writes=[(wo_.name, hh)], dma=True)
                        for blk in range(2):
                            nb_ = cg * 2 + blk
                            k = ei % 2
                            ei += 1
                            P.op("sp", lambda e, k=k, nb_=nb_, tsl=tsl: e.dma_start(out=xt[k][:], in_=xs_d[nb_ * 128:(nb_ + 1) * 128, tsl]), writes=[xt[k].name], dma=True)
                            P.op("pe", [lambda e, k=k, wo_=wo_, blk=blk, kc=kc: e.matmul(pso[k][:], lhsT=wo_[:, kc, blk * 128:(blk + 1) * 128], rhs=actT[:, kc, :], start=(kc == 0), stop=(kc == 43)) for kc in range(44)],
                                 reads=[(wo_.name, hh) for hh in range(4)] + [("actT", q) for q in range(44)], writes=[pso[k].name])
                            P.op("dve", lambda e, k=k, nb_=nb_: e.scalar_tensor_tensor(out=xt[k][:], in0=pso[k][:], scalar=modT[:, l, 80 + nb_:81 + nb_], in1=xt[k][:], op0=ALU.mult, op1=ALU.add),
                                 reads=[pso[k].name, xt[k].name, "modT"], writes=[xt[k].name])
                            P.op("sp", lambda e, k=k, nb_=nb_, tsl=tsl: e.dma_start(out=xs_d[nb_ * 128:(nb_ + 1) * 128, tsl], in_=xt[k][:]), reads=[xt[k].name], writes=[("xs", nb_ // 8, tt)], dma=True)
                P.flush()

        def stage_F():
            with ExitStack() as st:
                pools = norm_pools(st)
                ot = [st.enter_context(SBT("F_o%d" % i, [128, 16, 256], F32)) for i in range(2)]
                for ti in range(16):
                    o_ = ot[ti % 2]
                    norm_tiles(xs_d, fnw[:, :], zero16[:, :], lambda i, t0, n, o_=o_: (o_[:, i, :], (o_.name, i)), ti * 256, 256, pools, "nf")
                    for hh in range(2):
                        P.op("sp", lambda e, o_=o_, hh=hh, ti=ti: e.dma_start(out=outT[hh * 1024:(hh + 1) * 1024, ti * 256:(ti + 1) * 256].rearrange("(c p) t -> p c t", p=128), in_=o_[:, hh * 8:(hh + 1) * 8, :]),
                             reads=[(o_.name, i) for i in range(hh * 8, (hh + 1) * 8)], dma=True)
                P.flush()


        def stage_A(l, src_d):
                with ExitStack() as sa:
                    hT = sa.enter_context(SBT("hT", [128, 16, S], BF16))
                    with ExitStack() as sn:
                        pools = norm_pools(sn)
                        norm_tiles(src_d, a1[:, l, :], modT[:, l, 0:16], lambda i, t0, n: (hT[:, i, t0:t0 + n], ("hT", i, t0 // 512)), 0, S, pools, "n1")
                        if hT_dbg is not None and l == 0:
                            for i in range(16):
                                P.op("sp", lambda e, i=i: e.dma_start(out=hT_dbg[i * 128:(i + 1) * 128, :], in_=hT[:, i, :]),
                                     reads=[("hT", i, tt) for tt in range(8)], dma=True)
                        P.flush()
                    with ExitStack() as sw:
                        wb = [sw.enter_context(SBT("wb%d" % i, [128, 16, 512], BF16)) for i in range(2)]
                        stf = [sw.enter_context(SBT("stf%d" % i, [128, S], BF16)) for i in range(2)]
                        stt = [sw.enter_context(SBT("stt%d" % i, [128, 4, 512], BF16)) for i in range(2)]
                        stt32 = [sw.enter_context(SBT("stq%d" % i, [128, 4, 64], F32)) for i in range(2)]
                        psA = [sw.enter_context(PST("psA%d" % i, [128, 512], F32)) for i in range(6)]
                        segs = [("q", 0, 1536, "f", qT_d), ("k", 1536, 1536, "f", kT_d), ("v", 3072, 1536, "t", v_d),
                                ("z", 4608, 4096, "t", z_d), ("xbc", 8704, 6144, "f", xbcT_d), ("dt", 14848, 64, "t32", dtr_d),
                                ("ga", 14912, 2048, "f", gaT_d), ("gs", 16960, 2048, "f", gsT_d)]
                        wi = 0
                        pi = 0
                        fi = 0
                        ti_ = 0
                        ei = 0
                        for name, c0, ncols, mode, dest in segs:
                            for g0 in range(0, ncols, 512):
                                gw = min(512, ncols - g0)
                                wt = wb[wi % 2]
                                wi += 1
                                for hh in range(2):
                                    P.op("pool", lambda e, wt=wt, hh=hh, c0=c0, g0=g0, gw=gw, l=l: e.dma_start(
                                        out=wt[:, hh * 8:(hh + 1) * 8, 0:gw],
                                        in_=win_in[l, hh * 1024:(hh + 1) * 1024, c0 + g0:c0 + g0 + gw].rearrange("(c p) n -> p c n", p=128)),
                                        writes=[(wt.name, hh)], dma=True)
                                wres = [(wt.name, 0), (wt.name, 1)]
                                if mode == "f":
                                    for blk in range(gw // 128):
                                        sf = stf[fi % 2]
                                        fi += 1
                                        for tt in range(8):
                                            pt = psA[pi % 6]
                                            pi += 1
                                            P.op("pe", [lambda e, pt=pt, wt=wt, blk=blk, kc=kc, tt=tt: e.matmul(
                                                pt[:], lhsT=wt[:, kc, blk * 128:(blk + 1) * 128], rhs=hT[:, kc, tt * 512:(tt + 1) * 512],
                                                start=(kc == 0), stop=(kc == 15)) for kc in range(16)],
                                                reads=wres + [("hT", kc, tt) for kc in range(16)], writes=[pt.name])
                                            if ei % 2 == 0:
                                                P.op("act", lambda e, pt=pt, sf=sf, tt=tt: e.activation(out=sf[:, tt * 512:(tt + 1) * 512], in_=pt[:], func=AF.Copy),
                                                     reads=[pt.name], writes=[(sf.name, tt)])
                                            else:
                                                P.op("dve", lambda e, pt=pt, sf=sf, tt=tt: e.tensor_copy(out=sf[:, tt * 512:(tt + 1) * 512], in_=pt[:]),
                                                     reads=[pt.name], writes=[(sf.name, tt)])
                                            ei += 1
                                        r0 = g0 + blk * 128
                                        P.op("sp", lambda e, sf=sf, dest=dest, r0=r0: e.dma_start(out=dest[r0:r0 + 128, :], in_=sf[:]),
                                             reads=[(sf.name, tt) for tt in range(8)], dma=True)
                                else:
                                    for tb4 in range(8):
                                        st_ = (stt if mode == "t" else stt32)[ti_ % 2]
                                        ti_ += 1
                                        for q4 in range(4):
                                            tb = tb4 * 4 + q4
                                            pt = psA[pi % 6]
                                            pi += 1
                                            P.op("pe", [lambda e, pt=pt, wt=wt, kc=kc, tb=tb, gw=gw: e.matmul(
                                                pt[:, 0:gw], lhsT=hT[:, kc, tb * 128:(tb + 1) * 128], rhs=wt[:, kc, 0:gw],
                                                start=(kc == 0), stop=(kc == 15)) for kc in range(16)],
                                                reads=wres + [("hT", kc, tb // 4) for kc in range(16)], writes=[pt.name])
                                            if ei % 2 == 0:
                                                P.op("act", lambda e, pt=pt, st_=st_, q4=q4, gw=gw: e.activation(out=st_[:, q4, 0:gw], in_=pt[:, 0:gw], func=AF.Copy),
                                                     reads=[pt.name], writes=[(st_.name, q4)])
                                            else:
                                                P.op("dve", lambda e, pt=pt, st_=st_, q4=q4, gw=gw: e.tensor_copy(out=st_[:, q4, 0:gw], in_=pt[:, 0:gw]),
                                                     reads=[pt.name], writes=[(st_.name, q4)])
                                            ei += 1
                                        P.op("sp", lambda e, st_=st_, dest=dest, tb4=tb4, g0=g0, gw=gw: e.dma_start(
                                            out=dest[tb4 * 512:(tb4 + 1) * 512, g0:g0 + gw].rearrange("(q p) c -> p q c", p=128), in_=st_[:, :, 0:gw]),
                                            reads=[(st_.name, q4) for q4 in range(4)], dma=True)
                        P.flush()

        for l in range(nlayers):
            src_d = xT_in if l == 0 else xs_d
            if run("A"):
                stage_A(l, src_d)
            if run("B"):
                stage_B(l)
            if run("C") or run("C1") or run("C23"):
                stage_C(l)
            if run("D"):
                stage_D(l, src_d)
            if run("E"):
                stage_E(l)
        if run("F"):
            stage_F()
    return nc


def prep_shared(inputs):
    f = lambda a: np.ascontiguousarray(np.asarray(a, dtype=np.float32))
    sh = {}
    sh["rel_bias"] = f(inputs["rel_bias"])
    sh["n1w"] = f(inputs["norm1_w"].reshape(DEPTH, 16, 128).transpose(0, 2, 1))
    sh["n2w"] = f(inputs["norm2_w"].reshape(DEPTH, 16, 128).transpose(0, 2, 1))
    sh["fnw"] = f(inputs["final_norm_w"].reshape(16, 128).T)
    sh["w_mod"] = f(inputs["w_mod"])
    sh["bmodT"] = f(inputs["b_mod"].reshape(DEPTH, 96, 128).transpose(0, 2, 1))
    sh["w_in"] = f(inputs["w_in"])
    sh["convwT"] = f(inputs["conv_w"].reshape(DEPTH, 4, 48, 128).transpose(0, 3, 2, 1))
    sh["convbT"] = f(inputs["conv_b"].reshape(DEPTH, 48, 128).transpose(0, 2, 1))
    sh["dtb_b"] = f(np.broadcast_to(inputs["dt_bias"][:, None, :], (DEPTH, 128, 64)))
    sh["alog_b"] = f(np.broadcast_to(inputs["a_log"][:, None, :], (DEPTH, 128, 64)))
    sh["dsk_b"] = f(np.broadcast_to(np.repeat(inputs["d_skip"], 64, axis=1)[:, None, :], (DEPTH, 128, 4096)))
    sh["snwT"] = f(inputs["ssm_norm_w"].reshape(DEPTH, 32, 128).transpose(0, 2, 1))
    sh["w_attn_proj"] = f(inputs["w_attn_proj"])
    sh["w_ssm_proj"] = f(inputs["w_ssm_proj"])
    sh["w_out"] = f(inputs["w_out"])
    sh["w_ffn_in"] = f(inputs["w_ffn_in"])
    sh["w_ffn_out"] = f(inputs["w_ffn_out"])
    sh.update(host_consts())
    return sh


def prep_core(inputs, b, sh):
    m = dict(sh)
    m["xT"] = np.ascontiguousarray(np.asarray(inputs["x"][b], dtype=np.float32).T)
    m["cT"] = np.ascontiguousarray(np.asarray(inputs["c"][b], dtype=np.float32).reshape(16, 128).T)
    return m


_CACHE = {}


def kernel(**inputs):
    if "nc" not in _CACHE:
        _CACHE["nc"] = build()
    nc = _CACHE["nc"]
    sh = prep_shared(inputs)
    in_maps = [prep_core(inputs, i % 4, sh) for i in range(8)]
    res = run_bass_kernel_spmd(nc, in_maps, core_ids=list(range(8)))
    out = np.stack([np.ascontiguousarray(res.results[b]["outT"].T) for b in range(4)], axis=0)
    return out.astype(np.float32)
```

```python
import math
from contextlib import ExitStack
import numpy as np
import concourse.bass as bass
import concourse.mybir as mybir
from concourse.bass_utils import run_bass_kernel_spmd

F32 = mybir.dt.float32
BF16 = mybir.dt.bfloat16
AF = mybir.ActivationFunctionType
ALU = mybir.AluOpType
AX = mybir.AxisListType

D = 2048
S = 4096
DEPTH = 2
NIN = 19008
DFF = 5632
EPS = 1e-6
GROUPS = ((128, 1), (512, 4), (2048, 16))
NEG = -30000.0

ENGS = ["pe", "act", "dve", "pool", "sp"]
NDMA = 24
import os as _os
SAME_ENGINE_SYNC = _os.environ.get("KSES", "1") == "1"


class Prog:
    def __init__(self, nc, es):
        self.nc = nc
        self.sem = {e: es.enter_context(nc.semaphore("s_" + e)) for e in ENGS}
        self.cnt = {e: 0 for e in ENGS}
        self.dsem = [es.enter_context(nc.semaphore("d%d" % i)) for i in range(NDMA)]
        self.dcnt = [0] * NDMA
        self.ndma = 0
        self.dma_hist = []
        self.ops = []
        self.res = {}
        self.ninst = 0
        self.nops = 0
        self.maxops = int(_os.environ.get("KMAXOPS", "0")) or None

    def op(self, eng, fns, reads=(), writes=(), dma=False, n=256):
        if not isinstance(fns, (list, tuple)):
            fns = [fns]
        self.nops += 1
        if self.maxops is not None and self.nops > self.maxops:
            return None
        oid = len(self.ops)
        deps = set()
        for r in reads:
            st = self.res.get(r)
            if st is not None and st[0] is not None:
                deps.add(st[0])
        for w in writes:
            st = self.res.get(w)
            if st is not None:
                if st[0] is not None:
                    deps.add(st[0])
                deps.update(st[1])
        o = {"eng": eng, "fns": list(fns), "dma": dma, "deps": deps, "n": n}
        if dma:
            o["di"] = self.ndma % NDMA
            if len(self.dma_hist) >= NDMA:
                deps.add(self.dma_hist[-NDMA])
            self.dma_hist.append(oid)
            self.ndma += 1
        self.ops.append(o)
        self.ninst += len(fns)
        for w in writes:
            self.res[w] = [oid, set()]
        for r in reads:
            st = self.res.get(r)
            if st is None:
                st = self.res[r] = [None, set()]
            st[1].add(oid)
        return oid

    def _cost(self, o):
        e = o["eng"]
        if o["dma"]:
            return 0.6
        if e == "pe":
            return 0.07 * len(o["fns"]) + o["n"] / 2400.0
        if e == "pool":
            return 0.2 + o["n"] / 600.0
        return 0.15 + o["n"] / 1000.0

    def _schedule(self):
        ops = self.ops
        N = len(ops)
        HOP = 1.5
        W = int(_os.environ.get("KWIN", "24"))
        if W <= 1:
            return {e: [i for i in range(N) if ops[i]["eng"] == e] for e in ENGS}
        pend = {e: [i for i in range(N) if ops[i]["eng"] == e] for e in ENGS}
        fin = [None] * N
        tfree = {e: 0.0 for e in ENGS}
        order = {e: [] for e in ENGS}
        remaining = N
        while remaining:
            best = None
            for e in ENGS:
                lst = pend[e]
                for pos in range(min(W, len(lst))):
                    i = lst[pos]
                    o = ops[i]
                    rdy = 0.0
                    ok = True
                    for d in o["deps"]:
                        f = fin[d]
                        if f is None:
                            ok = False
                            break
                        if ops[d]["eng"] == e and e == "pe" and not ops[d]["dma"]:
                            lat = 0.0
                        else:
                            lat = HOP
                        if f + lat > rdy:
                            rdy = f + lat
                    if not ok:
                        continue
                    start = max(rdy, tfree[e])
                    key = (start, pos)
                    if best is None or key < best[0]:
                        best = (key, e, pos, i, start)
            assert best is not None, "scheduler deadlock"
            _, e, pos, i, start = best
            o = ops[i]
            c = self._cost(o)
            tfree[e] = start + c
            fin[i] = start + c + (2.5 if o["dma"] else 0.0)
            pend[e].pop(pos)
            order[e].append(i)
            remaining -= 1
        return order

    def flush(self):
        nc = self.nc
        ops = self.ops
        order = self._schedule()
        tokens = [None] * len(ops)
        for i, o in enumerate(ops):
            if o["dma"]:
                di = o["di"]
                self.dcnt[di] += 16
                tokens[i] = (("d", di), self.dcnt[di])
        for e in ENGS:
            for i in order[e]:
                if not ops[i]["dma"]:
                    self.cnt[e] += 1
                    tokens[i] = (("e", e), self.cnt[e])
        streams = {e: [] for e in ENGS}
        for e in ENGS:
            wd = {}
            for i in order[e]:
                o = ops[i]
                need = {}
                for d in o["deps"]:
                    k, v = tokens[d]
                    if k == ("e", e) and (e == "pe" or not SAME_ENGINE_SYNC):
                        continue
                    if need.get(k, 0) < v:
                        need[k] = v
                waits = []
                for k, v in need.items():
                    if wd.get(k, 0) >= v:
                        continue
                    wd[k] = v
                    waits.append((self.sem[k[1]] if k[0] == "e" else self.dsem[k[1]], v))
                k, v = tokens[i]
                inc = (self.dsem[k[1]], 16) if o["dma"] else (self.sem[e], 1)
                streams[e].append((waits, o["fns"], inc))
        fin_w = []
        for i in range(NDMA):
            if self.dcnt[i] > 0:
                fin_w.append((self.dsem[i], self.dcnt[i]))
        for e in ENGS:
            if e != "sp" and self.cnt[e] > 0:
                fin_w.append((self.sem[e], self.cnt[e]))
        streams["sp"].append((fin_w, [], None))
        self.ops = []
        self.res = {}
        self.dma_hist = []

        def replay(engobj, lst):
            for waits, fns, inc in lst:
                for s_, v in waits:
                    engobj.wait_ge(s_, v)
                ins = None
                for f in fns:
                    ins = f(engobj)
                if inc is not None and ins is not None:
                    ins.then_inc(inc[0], inc[1])

        with nc.Block() as block:
            @block.tensor
            def _(e):
                replay(e, streams["pe"])

            @block.scalar
            def _(e):
                replay(e, streams["act"])

            @block.vector
            def _(e):
                replay(e, streams["dve"])

            @block.gpsimd
            def _(e):
                replay(e, streams["pool"])

            @block.sync
            def _(e):
                replay(e, streams["sp"])


def t5_bucket_np(dist):
    exact = 16
    n = np.maximum(dist, 1).astype(np.float32)
    large = exact + (np.log(n / np.float32(exact)) / np.float32(math.log(2048 / exact)) * np.float32(32 - exact)).astype(np.int32)
    large = np.minimum(large, 31)
    return np.where(dist < exact, dist, large)


def host_consts():
    c = {}
    i = np.arange(128)
    c["tri"] = (i[:, None] <= i[None, :]).astype(np.float32)
    c["gst"] = (i[:, None] > i[None, :]).astype(np.float32)
    c["ones"] = np.ones((128, 128), np.float32)
    c["ident"] = np.eye(128, dtype=np.float32)
    oh = np.zeros((3, 33, 384), np.float32)
    for g, (win, dil) in enumerate(GROUPS):
        for ii in range(384):
            steps = 255 - ii
            if 0 <= steps <= 128:
                b = int(t5_bucket_np(np.array([steps * dil]))[0])
                oh[g, b, ii] = 1.0
            else:
                oh[g, 32, ii] = NEG
    c["ohu"] = oh
    return c


def build(dbg=(), nlayers=DEPTH, only=None, feed=(), skip=()):
    nc = bass.Bass("TRN2", target_bir_lowering=False)
    dbg = set(dbg)
    uid = [0]

    def SBT(name, shape, dt):
        uid[0] += 1
        return nc.sbuf_tensor("%s_%d" % (name, uid[0]), list(shape), dt)

    def PST(name, shape, dt):
        uid[0] += 1
        return nc.psum_tensor("%s_%d" % (name, uid[0]), list(shape), dt)

    feed = set(feed)
    skip = set(skip)

    def run(stage):
        return only is None or stage in only

    def din(name, shape, dt=F32):
        if name in skip:
            return nc.dram_tensor(name, [2, 2, 2], dt, kind="Internal").ap()
        return nc.dram_tensor(name, list(shape), dt, kind="ExternalInput").ap()

    def dscr(name, shape, dt):
        kind = "ExternalOutput" if name in dbg else ("ExternalInput" if name in feed else "Internal")
        return nc.dram_tensor(name, list(shape), dt, kind=kind).ap()

    xT_in = din("xT", [D, S])
    cT_in = din("cT", [128, 16])
    relb_in = din("rel_bias", [32, 12])
    n1w_in = din("n1w", [DEPTH, 128, 16])
    n2w_in = din("n2w", [DEPTH, 128, 16])
    fnw_in = din("fnw", [128, 16])
    wmod_in = din("w_mod", [DEPTH, D, 6 * D])
    bmod_in = din("bmodT", [DEPTH, 128, 96])
    win_in = din("w_in", [DEPTH, D, NIN])
    convw_in = din("convwT", [DEPTH, 128, 48, 4])
    convb_in = din("convbT", [DEPTH, 128, 48])
    dtb_in = din("dtb_b", [DEPTH, 128, 64])
    alog_in = din("alog_b", [DEPTH, 128, 64])
    dsk_in = din("dsk_b", [DEPTH, 128, 4096])
    snw_in = din("snwT", [DEPTH, 128, 32])
    wattn_in = din("w_attn_proj", [DEPTH, 512, D])
    wssm_in = din("w_ssm_proj", [DEPTH, 4096, D])
    wout_in = din("w_out", [DEPTH, D, D])
    wffi_in = din("w_ffn_in", [DEPTH, D, 2 * DFF])
    wffo_in = din("w_ffn_out", [DEPTH, DFF, D])
    tri_in = din("tri", [128, 128])
    gst_in = din("gst", [128, 128])
    ones_in = din("ones", [128, 128])
    ident_in = din("ident", [128, 128])
    ohu_in = din("ohu", [3, 33, 384])
    outT = nc.dram_tensor("outT", [D, S], F32, kind="ExternalOutput").ap()

    xs_d = dscr("xs", [D, S], F32)
    qT_d = dscr("qT", [1536, S], BF16)
    kT_d = dscr("kT", [1536, S], BF16)
    v_d = dscr("v", [S, 1536], BF16)
    z_d = dscr("z", [S, 4096], BF16)
    xbcT_d = dscr("xbcT", [6144, S], BF16)
    dtr_d = dscr("dtr", [S, 64], F32)
    gaT_d = dscr("gaT", [D, S], BF16)
    gsT_d = dscr("gsT", [D, S], BF16)
    ao_d = dscr("ao", [3, S, 4 * 129], F32)
    xst_d = dscr("xst", [S, 4096], BF16)
    bt_d = dscr("btok", [S, 1024], BF16)
    BT_d = dscr("BT", [1024, S], BF16)
    CT_d = dscr("CT", [1024, S], BF16)
    yT_d = dscr("yT", [4096, S], BF16)
    toep_t = nc.dram_tensor("toep", [12, 128 * 384], F32)
    toep_d = toep_t.ap()
    hT_dbg = dscr("hTd", [D, S], BF16) if "hTd" in dbg else None
    mod_dbg = dscr("modd", [DEPTH, 128, 96], F32) if "modd" in dbg else None

    with ExitStack() as es:
        P = Prog(nc, es)

        def sbp(name, shape, dt):
            return es.enter_context(SBT("sb_" + name, list(shape), dt))

        tri = sbp("tri", [128, 128], F32)
        gst = sbp("gst", [128, 128], F32)
        ones = sbp("ones", [128, 128], F32)
        identf = sbp("identf", [128, 128], F32)
        ident = sbp("ident", [128, 128], BF16)
        modT = sbp("modT", [128, DEPTH, 96], F32)
        a1 = sbp("a1", [128, DEPTH, 16], F32)
        a2 = sbp("a2", [128, DEPTH, 16], F32)
        n1w = sbp("n1w", [128, DEPTH, 16], F32)
        n2w = sbp("n2w", [128, DEPTH, 16], F32)
        fnw = sbp("fnw", [128, 16], F32)
        zero16 = sbp("zero16", [128, 16], F32)
        cact = sbp("cact", [128, 16], F32)

        for t, src, key in ((tri, tri_in, "tri"), (gst, gst_in, "gst"), (ones, ones_in, "ones"), (identf, ident_in, "identf"), (fnw, fnw_in, "fnw"), (cact, cT_in, "cact")):
            P.op("sp", lambda e, t=t, src=src: e.dma_start(out=t[:], in_=src), writes=[key], dma=True)
        P.op("sp", lambda e: e.dma_start(out=n1w[:], in_=n1w_in.rearrange("l p i -> p l i")), writes=["n1w"], dma=True)
        P.op("sp", lambda e: e.dma_start(out=n2w[:], in_=n2w_in.rearrange("l p i -> p l i")), writes=["n2w"], dma=True)
        P.op("dve", lambda e: e.tensor_copy(out=ident[:], in_=identf[:]), reads=["identf"], writes=["ident"])
        P.op("dve", lambda e: e.memset(zero16[:], 0.0), writes=["zero16"])
        P.op("act", lambda e: e.activation(out=cact[:], in_=cact[:], func=AF.Silu), reads=["cact"], writes=["cact"])

        if not run("M"):
            P.flush()
        with ExitStack() as ms:
            if not run("M"):
                nlm = 0
            else:
                nlm = nlayers
            wm = [ms.enter_context(SBT("wm%d" % i, [128, 16, 512], F32)) for i in range(2)]
            bm = ms.enter_context(SBT("bm", [128, DEPTH, 96], F32))
            psm = ms.enter_context(PST("psm", [128, DEPTH * 96], F32))
            P.op("sp", lambda e: e.dma_start(out=bm[:], in_=bmod_in.rearrange("l p j -> p l j")), writes=["bm"], dma=True)
            it = 0
            for l in range(nlm):
                for gcol in range(24):
                    wt = wm[it % 2]
                    for hh in range(2):
                        P.op("sp", lambda e, wt=wt, l=l, gcol=gcol, hh=hh: e.dma_start(
                            out=wt[:, hh * 8:(hh + 1) * 8, :],
                            in_=wmod_in[l, hh * 1024:(hh + 1) * 1024, gcol * 512:(gcol + 1) * 512].rearrange("(c p) n -> p c n", p=128)),
                            writes=[(wt.name, hh)], dma=True)
                    fns = []
                    for blk in range(4):
                        j = gcol * 4 + blk
                        for kc in range(16):
                            fns.append(lambda e, wt=wt, blk=blk, kc=kc, j=j, l=l: e.matmul(
                                psm[:, l * 96 + j:l * 96 + j + 1], lhsT=wt[:, kc, blk * 128:(blk + 1) * 128], rhs=cact[:, kc:kc + 1],
                                start=(kc == 0), stop=(kc == 15)))
                    P.op("pe", fns, reads=[(wt.name, 0), (wt.name, 1), "cact"], writes=["psm"])
                    it += 1
                P.op("dve", lambda e, l=l: e.tensor_tensor(out=modT[:, l, :], in0=psm[:, l * 96:(l + 1) * 96], in1=bm[:, l, :], op=ALU.add),
                     reads=["psm", "bm"], writes=["modT"])
                P.op("dve", lambda e, l=l: e.scalar_tensor_tensor(out=a1[:, l, :], in0=modT[:, l, 16:32], scalar=1.0, in1=n1w[:, l, :], op0=ALU.add, op1=ALU.mult),
                     reads=["modT", "n1w"], writes=["a1"])
                P.op("dve", lambda e, l=l: e.scalar_tensor_tensor(out=a2[:, l, :], in0=modT[:, l, 64:80], scalar=1.0, in1=n2w[:, l, :], op0=ALU.add, op1=ALU.mult),
                     reads=["modT", "n2w"], writes=["a2"])
            if mod_dbg is not None:
                P.op("sp", lambda e: e.dma_start(out=mod_dbg.rearrange("l p j -> p l j"), in_=modT[:]), reads=["modT"], dma=True)
            P.flush()

        def norm_tiles(src_d, avec, bvec, dst_fn, tok0, ntok, pools, tag):
            xt, sq, rsb, tmpb, psn = pools
            nt = ntok // 256
            for ti in range(nt):
                t0 = tok0 + ti * 256
                xb = xt[ti % 2]
                for hh in range(2):
                    P.op("sp", lambda e, xb=xb, hh=hh, t0=t0: e.dma_start(
                        out=xb[:, hh * 8:(hh + 1) * 8, :], in_=src_d[hh * 1024:(hh + 1) * 1024, t0:t0 + 256].rearrange("(c p) t -> p c t", p=128)),
                        writes=[(xb.name, hh)], dma=True)
                pn = psn[ti % 2]
                for i in range(16):
                    sqb = sq[i % 2]
                    P.op("act", lambda e, sqb=sqb, xb=xb, i=i: e.activation(out=sqb[:], in_=xb[:, i, :], func=AF.Square),
                         reads=[(xb.name, i // 8)], writes=[sqb.name])
                    P.op("pe", lambda e, sqb=sqb, pn=pn, i=i: e.matmul(pn[:], lhsT=ones[:], rhs=sqb[:], start=(i == 0), stop=(i == 15)),
                         reads=[sqb.name, "ones"], writes=[pn.name])
                rs = rsb[ti % 2]
                P.op("dve", lambda e, rs=rs, pn=pn: e.tensor_scalar(out=rs[:], in0=pn[:], scalar1=1.0 / D, scalar2=EPS, op0=ALU.mult, op1=ALU.add),
                     reads=[pn.name], writes=[rs.name])
                P.op("act", lambda e, rs=rs: e.activation(out=rs[:], in_=rs[:], func=AF.Sqrt), reads=[rs.name], writes=[rs.name])
                P.op("dve", lambda e, rs=rs: e.reciprocal(out=rs[:], in_=rs[:]), reads=[rs.name], writes=[rs.name])
                for i in range(16):
                    tb = tmpb[i % 2]
                    P.op("dve", lambda e, tb=tb, xb=xb, rs=rs, i=i: e.tensor_tensor(out=tb[:], in0=xb[:, i, :], in1=rs[:], op=ALU.mult),
                         reads=[(xb.name, i // 8), rs.name], writes=[tb.name])
                    dst, dres = dst_fn(i, t0, 256)
                    P.op("act", lambda e, tb=tb, dst=dst, i=i: e.activation(out=dst, in_=tb[:], func=AF.Identity, bias=bvec[:, i:i + 1], scale=avec[:, i:i + 1]),
                         reads=[tb.name, "a1", "a2", "modT"], writes=[dres])

        def norm_pools(st):
            xt = [st.enter_context(SBT("nx%d" % i, [128, 16, 256], F32)) for i in range(2)]
            sq = [st.enter_context(SBT("nsq%d" % i, [128, 256], F32)) for i in range(2)]
            rsb = [st.enter_context(SBT("nrs%d" % i, [128, 256], F32)) for i in range(2)]
            tmpb = [st.enter_context(SBT("ntm%d" % i, [128, 256], F32)) for i in range(2)]
            psn = [st.enter_context(PST("psn%d" % i, [128, 256], F32)) for i in range(2)]
            return xt, sq, rsb, tmpb, psn

        def stage_B(l):
            with ExitStack() as st:
                def sb(name, shape, dt):
                    return st.enter_context(SBT("B_" + name, list(shape), dt))

                def pp(name, shape, dt):
                    return st.enter_context(PST("Bp_" + name, list(shape), dt))
                psU = pp("U", [128, 512], F32)
                psS = [pp("S%d" % i, [128, 256], F32) for i in range(2)]
                psT = [pp("T%d" % i, [128, 256], BF16) for i in range(2)]
                psO = [pp("O%d" % i, [128, 128], F32) for i in range(2)]
                QK = [sb("qk%d" % i, [128, S], BF16) for i in range(8)]
                Vg = sb("Vg", [128, 32, 512], BF16)
                BM = sb("BM", [128, 4, 256], F32)
                rbt = sb("rbt", [32, 12], F32)
                RB = sb("RB", [33, 4, 128], F32)
                oht = sb("oht", [33, 384], F32)
                Ub = sb("Ub", [128, 384], F32)
                ssb = [sb("ss%d" % i, [128, 256], F32) for i in range(2)]
                pb = [sb("pb%d" % i, [128, 256], BF16) for i in range(2)]
                ptb = [sb("pt%d" % i, [128, 256], BF16) for i in range(2)]
                ol = [sb("ol%d" % i, [128, 4, 129], F32) for i in range(2)]
                stt = [sb("st%d" % i, [128, 8], F32) for i in range(4)]
                P.op("sp", lambda e: e.dma_start(out=rbt[:], in_=relb_in), writes=["rbt"], dma=True)
                it = 0
                bi = 0
                for g, (win, dil) in enumerate(GROUPS):
                    nb = 32 // dil
                    P.op("sp", lambda e, g=g: e.dma_start(out=oht[:], in_=ohu_in[g]), writes=["oht"], dma=True)
                    P.op("pool", lambda e: e.memset(RB[32:33, :, :], 1.0), writes=["RB1"])
                    for j in range(4):
                        P.op("dve", lambda e, j=j, g=g: e.tensor_scalar(out=RB[0:32, j, :], in0=ones[0:32, :], scalar1=rbt[0:32, g * 4 + j:g * 4 + j + 1], scalar2=None, op0=ALU.mult),
                             reads=["rbt", "ones"], writes=[("RB", j)])
                        P.op("pe", lambda e, j=j: e.matmul(psU[:, 0:384], lhsT=RB[0:33, j, :], rhs=oht[0:33, :], start=True, stop=True),
                             reads=[("RB", j), "RB1", "oht"], writes=["psU"])
                        P.op("dve", lambda e: e.tensor_copy(out=Ub[:], in_=psU[:, 0:384]), reads=["psU"], writes=["Ub"])
                        hd = g * 4 + j
                        P.op("sp", lambda e, hd=hd: e.dma_start(out=toep_d[hd].rearrange("(p i) -> p i", p=128), in_=Ub[:]), reads=["Ub"], writes=[("toep", hd)], dma=True)
                        P.op("sp", lambda e, hd=hd, j=j: e.dma_start(out=BM[:, j, :], in_=bass.AP(toep_t, hd * 128 * 384 + 127, [[383, 128], [1, 256]])),
                             reads=[("toep", hd)], writes=[("BM", j)], dma=True)
                    for j in range(4):
                        hd = g * 4 + j
                        P.op("sp", lambda e, j=j, hd=hd: e.dma_start(out=QK[j][:], in_=qT_d[hd * 128:(hd + 1) * 128, :]), writes=[("Q", j)], dma=True)
                        P.op("sp", lambda e, j=j, hd=hd: e.dma_start(out=QK[4 + j][:], in_=kT_d[hd * 128:(hd + 1) * 128, :]), writes=[("K", j)], dma=True)
                    vsrc = v_d[:, g * 512:(g + 1) * 512].rearrange("(b p r) c -> p b r c", p=128, r=dil)
                    Vv = Vg[:].rearrange("p (b r) c -> p b r c", r=dil)
                    for r in range(dil):
                        P.op("sp", lambda e, r=r, vsrc=vsrc, Vv=Vv: e.dma_start(out=Vv[:, :, r, :], in_=vsrc[:, :, r, :]), writes=["Vtile"], dma=True)
                    for r in range(dil):
                        for b in range(nb):
                            olb = ol[bi % 2]
                            bi += 1
                            for j in range(4):
                                q0 = b * 128 * dil + r
                                if b == 0:
                                    nk, k0, koff = 128, r, 128
                                else:
                                    nk, k0, koff = 256, (b - 1) * 128 * dil + r, 0
                                qsl = QK[j][:, q0:q0 + 127 * dil + 1:dil]
                                ksl = QK[4 + j][:, k0:k0 + (nk - 1) * dil + 1:dil]
                                pS, sS, pB, pT, pTb, pO, sa = psS[it % 2], ssb[it % 2], pb[it % 2], psT[it % 2], ptb[it % 2], psO[it % 2], stt[it % 4]
                                it += 1
                                P.op("pe", lambda e, pS=pS, qsl=qsl, ksl=ksl, nk=nk: e.matmul(pS[:, 0:nk], lhsT=qsl, rhs=ksl, start=True, stop=True),
                                     reads=[("Q", j), ("K", j)], writes=[pS.name])
                                P.op("dve", lambda e, pS=pS, sS=sS, nk=nk, j=j, koff=koff: e.scalar_tensor_tensor(
                                    out=sS[:, 0:nk], in0=pS[:, 0:nk], scalar=1.0 / math.sqrt(128.0), in1=BM[:, j, koff:koff + nk], op0=ALU.mult, op1=ALU.add),
                                    reads=[pS.name, ("BM", j)], writes=[sS.name])
                                P.op("dve", lambda e, sS=sS, sa=sa, nk=nk: e.reduce_max(out=sa[:, 0:1], in_=sS[:, 0:nk], axis=AX.X), reads=[sS.name], writes=[(sa.name, 0)])
                                P.op("dve", lambda e, sa=sa: e.tensor_scalar(out=sa[:, 1:2], in0=sa[:, 0:1], scalar1=-1.0, scalar2=None, op0=ALU.mult),
                                     reads=[(sa.name, 0)], writes=[(sa.name, 1)])
                                P.op("act", lambda e, sS=sS, pB=pB, sa=sa, nk=nk: e.activation(out=pB[:, 0:nk], in_=sS[:, 0:nk], func=AF.Exp, bias=sa[:, 1:2], scale=1.0, accum_out=sa[:, 2:3]),
                                     reads=[sS.name, (sa.name, 1)], writes=[pB.name, (sa.name, 2)])
                                P.op("pe", [lambda e, pT=pT, pB=pB, i=i: e.transpose(pT[:, i * 128:(i + 1) * 128], pB[:, i * 128:(i + 1) * 128], ident[:]) for i in range(nk // 128)],
                                     reads=[pB.name, "ident"], writes=[pT.name])
                                P.op("act", lambda e, pT=pT, pTb=pTb, nk=nk: e.activation(out=pTb[:, 0:nk], in_=pT[:, 0:nk], func=AF.Copy), reads=[pT.name], writes=[pTb.name])
                                vb0 = b if b == 0 else b - 1
                                P.op("pe", [lambda e, pO=pO, pTb=pTb, i=i, vb0=vb0, r=r, j=j, nk=nk, Vv=Vv: e.matmul(
                                    pO[:], lhsT=pTb[:, i * 128:(i + 1) * 128], rhs=Vv[:, vb0 + i, r, j * 128:(j + 1) * 128], start=(i == 0), stop=(i == nk // 128 - 1))
                                    for i in range(nk // 128)], reads=[pTb.name, "Vtile"], writes=[pO.name])
                                P.op("dve", lambda e, sa=sa: e.reciprocal(out=sa[:, 3:4], in_=sa[:, 2:3]), reads=[(sa.name, 2)], writes=[(sa.name, 3)])
                                P.op("dve", lambda e, pO=pO, olb=olb, j=j, sa=sa: e.tensor_scalar(out=olb[:, j, 0:128], in0=pO[:], scalar1=sa[:, 3:4], scalar2=None, op0=ALU.mult),
                                     reads=[pO.name, (sa.name, 3)], writes=[(olb.name, j)])
                                P.op("act", lambda e, sa=sa: e.activation(out=sa[:, 4:5], in_=sa[:, 2:3], func=AF.Ln), reads=[(sa.name, 2)], writes=[(sa.name, 4)])
                                P.op("pool", lambda e, sa=sa, olb=olb, j=j: e.tensor_tensor(out=olb[:, j, 128:129], in0=sa[:, 4:5], in1=sa[:, 0:1], op=ALU.add),
                                     reads=[(sa.name, 4), (sa.name, 0)], writes=[(olb.name, ("l", j))])
                            dst = ao_d[g].rearrange("(b p r) c -> p b r c", p=128, r=dil)[:, b, r, :]
                            P.op("sp", lambda e, dst=dst, olb=olb: e.dma_start(out=dst, in_=olb[:].rearrange("p j c -> p (j c)")),
                                 reads=[(olb.name, j) for j in range(4)] + [(olb.name, ("l", j)) for j in range(4)], dma=True)
                P.flush()

        def stage_C(l):
            if run("C1") or run("C"):
                stage_C1(l)
            if run("C23") or run("C"):
                stage_C23(l)

        def stage_C1(l):
            with ExitStack() as st:
                def sb(name, shape, dt):
                    return st.enter_context(SBT("C1_" + name, list(shape), dt))
                u4 = [sb("u%d" % i, [128, 4, S + 3], BF16) for i in range(2)]
                acc = [sb("acc%d" % i, [128, S], F32) for i in range(2)]
                cv4 = [sb("cv%d" % i, [128, 4, S], BF16) for i in range(2)]
                tst = [sb("tst%d" % i, [128, 8, 512], BF16) for i in range(2)]
                cw = sb("cw", [128, 48, 4], F32)
                cbias = sb("cb", [128, 48], F32)
                psT = [st.enter_context(PST("C1p%d" % i, [128, 512], BF16)) for i in range(4)]
                P.op("sp", lambda e: e.dma_start(out=cw[:], in_=convw_in[l]), writes=["cw"], dma=True)
                P.op("sp", lambda e: e.dma_start(out=cbias[:], in_=convb_in[l]), writes=["cbias"], dma=True)
                for i in range(2):
                    P.op("pool", lambda e, i=i: e.memset(u4[i][:, :, 0:3], 0.0), writes=[(u4[i].name, "pad")])
                ti = 0
                pi = 0
                ai = 0
                for G in range(12):
                    ub, cvb = u4[G % 2], cv4[G % 2]
                    P.op("sp", lambda e, ub=ub, G=G: e.dma_start(out=ub[:, :, 3:3 + S], in_=xbcT_d[G * 512:(G + 1) * 512, :].rearrange("(k p) t -> p k t", p=128)),
                         writes=[(ub.name, "d")], dma=True)
                    for k in range(4):
                        cb = G * 4 + k
                        ac = acc[ai % 2]
                        ai += 1
                        P.op("dve", lambda e, ac=ac, ub=ub, k=k, cb=cb: e.tensor_scalar(out=ac[:], in0=ub[:, k, 0:S], scalar1=cw[:, cb, 0:1], scalar2=None, op0=ALU.mult),
                             reads=[(ub.name, "d"), (ub.name, "pad"), "cw"], writes=[ac.name])
                        for jj in range(1, 4):
                            eng = "dve"
                            P.op(eng, lambda e, ac=ac, ub=ub, k=k, cb=cb, jj=jj: e.scalar_tensor_tensor(
                                out=ac[:], in0=ub[:, k, jj:jj + S], scalar=cw[:, cb, jj:jj + 1], in1=ac[:], op0=ALU.mult, op1=ALU.add),
                                reads=[(ub.name, "d"), (ub.name, "pad"), "cw", ac.name], writes=[ac.name])
                        P.op("act", lambda e, ac=ac, cvb=cvb, k=k, cb=cb: e.activation(out=cvb[:, k, :], in_=ac[:], func=AF.Silu, bias=cbias[:, cb:cb + 1], scale=1.0),
                             reads=[ac.name, "cbias"], writes=[(cvb.name, k)])
                    if G < 10:
                        dest, coff = (xst_d, G * 512) if G < 8 else (bt_d, (G - 8) * 512)
                        for c8 in range(4):
                            ts_ = tst[ti % 2]
                            ti += 1
                            for q in range(8):
                                c = c8 * 8 + q
                                pt = psT[pi % 4]
                                pi += 1
                                P.op("pe", [lambda e, pt=pt, cvb=cvb, k=k, c=c: e.transpose(pt[:, k * 128:(k + 1) * 128], cvb[:, k, c * 128:(c + 1) * 128], ident[:]) for k in range(4)],
                                     reads=[(cvb.name, k) for k in range(4)] + ["ident"], writes=[pt.name])
                                if q % 2 == 0:
                                    P.op("act", lambda e, pt=pt, ts_=ts_, q=q: e.activation(out=ts_[:, q, :], in_=pt[:], func=AF.Copy), reads=[pt.name], writes=[(ts_.name, q)])
                                else:
                                    P.op("dve", lambda e, pt=pt, ts_=ts_, q=q: e.tensor_copy(out=ts_[:, q, :], in_=pt[:]), reads=[pt.name], writes=[(ts_.name, q)])
                            P.op("sp", lambda e, ts_=ts_, dest=dest, coff=coff, c8=c8: e.dma_start(
                                out=dest[c8 * 1024:(c8 + 1) * 1024, coff:coff + 512].rearrange("(q p) c -> p q c", p=128), in_=ts_[:]),
                                reads=[(ts_.name, q) for q in range(8)], dma=True)
                    if G >= 8:
                        dest, r0 = (BT_d, (G - 8) * 512) if G < 10 else (CT_d, (G - 10) * 512)
                        P.op("sp", lambda e, cvb=cvb, dest=dest, r0=r0: e.dma_start(out=dest[r0:r0 + 512, :].rearrange("(k p) t -> p k t", p=128), in_=cvb[:]),
                             reads=[(cvb.name, k) for k in range(4)], dma=True)
                P.flush()
        def stage_C23(l):
            with ExitStack() as st:
                def sb(name, shape, dt):
                    return st.enter_context(SBT("C3_" + name, list(shape), dt))
                dt_sb = sb("dt", [128, 32, 64], F32)
                dtA = sb("dtA", [128, 32, 64], F32)
                expA = sb("expA", [128, 32, 64], F32)
                cdec = sb("cdec", [128, 32, 64], F32)
                dtdec = sb("dtdec", [128, 32, 64], F32)
                Ab = sb("Ab", [128, 64], F32)
                dtb = sb("dtb", [128, 64], F32)
                dsk = sb("dsk", [128, 4096], F32)
                snw = sb("snw", [128, 32], F32)
                hst = sb("hst", [128, 8, 512], F32)
                hbf = sb("hbf", [128, 8, 512], BF16)
                with ExitStack() as s2:
                    psA = s2.enter_context(PST("C2pA", [128, 2048], F32))
                    psL = s2.enter_context(PST("C2pL", [128, 2048], F32))
                    P.op("sp", lambda e: e.dma_start(out=dt_sb[:], in_=dtr_d.rearrange("(c p) h -> p c h", p=128)), writes=["dt"], dma=True)
                    P.op("sp", lambda e: e.dma_start(out=Ab[:], in_=alog_in[l]), writes=["Ab"], dma=True)
                    P.op("sp", lambda e: e.dma_start(out=dtb[:], in_=dtb_in[l]), writes=["dtb"], dma=True)
                    P.op("sp", lambda e: e.dma_start(out=dsk[:], in_=dsk_in[l]), writes=["dsk"], dma=True)
                    P.op("sp", lambda e: e.dma_start(out=snw[:], in_=snw_in[l]), writes=["snw"], dma=True)
                    bc = lambda t: t[:].unsqueeze(1).to_broadcast([128, 32, 64])
                    P.op("dve", lambda e: e.tensor_tensor(out=dt_sb[:], in0=dt_sb[:], in1=bc(dtb), op=ALU.add), reads=["dt", "dtb"], writes=["dt"])
                    P.op("act", lambda e: e.activation(out=expA[:], in_=dt_sb[:], func=AF.Abs), reads=["dt"], writes=["expA"])
                    P.op("act", lambda e: e.activation(out=expA[:], in_=expA[:], func=AF.Exp, scale=-1.0), reads=["expA"], writes=["expA"])
                    P.op("act", lambda e: e.activation(out=expA[:], in_=expA[:], func=AF.Ln, bias=1.0, scale=1.0), reads=["expA"], writes=["expA"])
                    P.op("dve", lambda e: e.tensor_scalar_max(out=dt_sb[:], in0=dt_sb[:], scalar1=0.0), reads=["dt"], writes=["dt"])
                    P.op("dve", lambda e: e.tensor_tensor(out=dt_sb[:], in0=dt_sb[:], in1=expA[:], op=ALU.add), reads=["dt", "expA"], writes=["dt"])
                    P.op("act", lambda e: e.activation(out=Ab[:], in_=Ab[:], func=AF.Exp), reads=["Ab"], writes=["Ab"])
                    P.op("dve", lambda e: e.tensor_scalar(out=Ab[:], in0=Ab[:], scalar1=-1.0, scalar2=None, op0=ALU.mult), reads=["Ab"], writes=["Ab"])
                    P.op("dve", lambda e: e.tensor_tensor(out=dtA[:], in0=dt_sb[:], in1=bc(Ab), op=ALU.mult), reads=["dt", "Ab"], writes=["dtA"])
                    P.op("pe", [lambda e, c=c: e.matmul(psA[:, c * 64:(c + 1) * 64], lhsT=tri[:], rhs=dtA[:, c, :], start=True, stop=True) for c in range(32)],
                         reads=["dtA", "tri"], writes=["psA"])
                    P.op("pe", [lambda e, c=c: e.matmul(psL[:, c * 64:(c + 1) * 64], lhsT=ones[:], rhs=dtA[:, c, :], start=True, stop=True) for c in range(32)],
                         reads=["dtA", "ones"], writes=["psL"])
                    fl = lambda t: t[:].rearrange("p c h -> p (c h)")
                    for bk_ in range(4):
                        bs = slice(bk_ * 512, (bk_ + 1) * 512)
                        P.op("dve", lambda e, bs=bs: e.tensor_copy(out=fl(dtdec)[:, bs], in_=psA[:, bs]), reads=[], writes=[("dtdec", bk_), "psA", "psL"])
                        P.op("act", lambda e, bs=bs: e.activation(out=fl(expA)[:, bs], in_=psA[:, bs], func=AF.Exp), reads=["dt", "expA"], writes=[("expA", bk_), "psA", "psL"])
                        P.op("act", lambda e, bs=bs: e.activation(out=fl(cdec)[:, bs], in_=psL[:, bs], func=AF.Exp), reads=[], writes=[("cdec", bk_), "psA", "psL"])
                        P.op("dve", lambda e, bs=bs: e.tensor_tensor(out=fl(dtdec)[:, bs], in0=psL[:, bs], in1=fl(dtdec)[:, bs], op=ALU.subtract), reads=[("dtdec", bk_)], writes=[("dtdec", bk_), "psA", "psL"])
                    P.op("act", lambda e: e.activation(out=fl(dtdec), in_=fl(dtdec), func=AF.Exp), reads=[("dtdec", b_) for b_ in range(4)], writes=["dtdec"])
                    P.op("dve", lambda e: e.tensor_tensor(out=dtdec[:], in0=dtdec[:], in1=dt_sb[:], op=ALU.mult), reads=["dtdec", "dt"], writes=["dtdec"])
                    P.op("dve", lambda e: e.memset(hst[:], 0.0), writes=["hst"])
                    P.op("pool", lambda e: e.memset(hbf[:], 0.0), writes=["hbf"])
                    P.flush()
                if only is not None and "noC3" in only:
                    return
                BTs = [sb("BTs%d" % i, [128, 8, 512], BF16) for i in range(2)]
                CTs = [sb("CTs%d" % i, [128, 8, 512], BF16) for i in range(2)]
                xst = [sb("xst%d" % i, [128, 4096], BF16) for i in range(2)]
                btk = [sb("btk%d" % i, [128, 1024], BF16) for i in range(2)]
                zt = [sb("zt%d" % i, [128, 512], BF16) for i in range(2)]
                X = [sb("X%d" % i, [128, 8, 128], F32) for i in range(2)]
                Ld = [sb("Ld%d" % i, [128, 8, 128], F32) for i in range(2)]
                MT = [sb("MT%d" % i, [128, 8, 128], BF16) for i in range(2)]
                cbm = [sb("cbm%d" % i, [128, 128], F32) for i in range(2)]
                xdt = [sb("xdt%d" % i, [128, 512], BF16) for i in range(2)]
                xdd = [sb("xdd%d" % i, [128, 512], BF16) for i in range(2)]
                t1 = [sb("t1%d" % i, [128, 512], F32) for i in range(2)]
                t2 = [sb("t2%d" % i, [128, 512], F32) for i in range(2)]
                t3 = [sb("t3%d" % i, [128, 512], F32) for i in range(2)]
                szt = [sb("sz%d" % i, [128, 512], F32) for i in range(2)]
                junk = sb("junk", [128, 512], F32)
                yg = [sb("yg%d" % i, [128, 512], F32) for i in range(2)]
                yn = [sb("yn%d" % i, [128, 512], BF16) for i in range(2)]
                yts = [sb("yts%d" % i, [128, 4, 128], BF16) for i in range(2)]
                sst = [sb("sst%d" % i, [128, 4], F32) for i in range(4)]
                psD = st.enter_context(PST("C3pD", [128, 1024], F32))
                psY = st.enter_context(PST("C3pY", [128, 512], F32))
                psYo = st.enter_context(PST("C3pYo", [128, 512], F32))
                psSt = st.enter_context(PST("C3pS", [128, 512], F32))
                psTr = st.enter_context(PST("C3pT", [128, 512], BF16))
                psCb = st.enter_context(PST("C3pC", [128, 128], F32))
                b8 = lambda ap: ap.unsqueeze(2).to_broadcast([128, 8, 64])
                v3 = lambda ap: ap.rearrange("p (r q) -> p r q", q=64)
                its = [(sc, cc, g) for sc in range(8) for cc in range(4) for g in range(8)]
                NIT = len(its)

                def ctx(i):
                    sc, cc, g = its[i]
                    c = sc * 4 + cc
                    return dict(sc=sc, cc=cc, g=g, c=c, k=i % 2, Bs=BTs[sc % 2], Cs=CTs[sc % 2], xs_=xst[c % 2], bk=btk[c % 2], z_=zt[i % 2],
                                tsl=slice(cc * 128, (cc + 1) * 128), hs=slice(g * 8, (g + 1) * 8), sa=sst[i % 4])

                def front(i):
                    x = ctx(i)
                    sc, cc, g, c, k, Bs, Cs, xs_, bk, z_, tsl, hs = (x[n] for n in ("sc", "cc", "g", "c", "k", "Bs", "Cs", "xs_", "bk", "z_", "tsl", "hs"))
                    if cc == 0 and g == 0:
                        P.op("sp", lambda e: e.dma_start(out=Bs[:], in_=BT_d[:, sc * 512:(sc + 1) * 512].rearrange("(g n) t -> n g t", n=128)), writes=[Bs.name], dma=True)
                        P.op("sp", lambda e: e.dma_start(out=Cs[:], in_=CT_d[:, sc * 512:(sc + 1) * 512].rearrange("(g n) t -> n g t", n=128)), writes=[Cs.name], dma=True)
                    if g == 0:
                        P.op("sp", lambda e: e.dma_start(out=xs_[:], in_=xst_d[c * 128:(c + 1) * 128, :]), writes=[xs_.name], dma=True)
                        P.op("sp", lambda e: e.dma_start(out=bk[:], in_=bt_d[c * 128:(c + 1) * 128, :]), writes=[bk.name], dma=True)
                    P.op("sp", lambda e: e.dma_start(out=z_[:], in_=z_d[c * 128:(c + 1) * 128, g * 512:(g + 1) * 512]), writes=[z_.name], dma=True)
                    P.op("pe", lambda e: e.matmul(psCb[:], lhsT=Bs[:, g, tsl], rhs=Cs[:, g, tsl], start=True, stop=True),
                         reads=[Bs.name, Cs.name], writes=["psCb"])
                    P.op("dve", lambda e: e.tensor_tensor(out=cbm[k][:], in0=psCb[:], in1=tri[:], op=ALU.mult), reads=["tri"], writes=[cbm[k].name, "psCb"])
                    P.op("pool", lambda e: e.tensor_tensor(out=X[k][:], in0=tri[:].unsqueeze(1).to_broadcast([128, 8, 128]),
                                                           in1=dtA[:, c, hs].unsqueeze(2).to_broadcast([128, 8, 128]), op=ALU.mult),
                         reads=["tri", "dtA"], writes=[X[k].name])
                    P.op("pe", [lambda e, h2=h2: e.matmul(psD[:, h2 * 512:(h2 + 1) * 512], lhsT=gst[:], rhs=X[k][:, h2 * 4:(h2 + 1) * 4, :].rearrange("p r l -> p (r l)"), start=True, stop=True)
                                for h2 in range(2)], reads=[X[k].name, "gst"], writes=["psD"])
                    for h2 in range(2):
                        P.op("act", lambda e, h2=h2: e.activation(out=Ld[k][:, h2 * 4:(h2 + 1) * 4, :].rearrange("p r l -> p (r l)"), in_=psD[:, h2 * 512:(h2 + 1) * 512], func=AF.Exp),
                             reads=[], writes=[(Ld[k].name, h2), "psD"])
                    P.op("dve", lambda e: e.tensor_tensor(out=MT[k][:], in0=Ld[k][:], in1=cbm[k][:].unsqueeze(1).to_broadcast([128, 8, 128]), op=ALU.mult),
                         reads=[(Ld[k].name, 0), (Ld[k].name, 1), cbm[k].name], writes=[MT[k].name])
                    xg = xs_[:, g * 512:(g + 1) * 512].rearrange("p (r q) -> p r q", q=64)
                    P.op("pool", lambda e: e.tensor_tensor(out=v3(xdt[k][:]), in0=xg, in1=b8(dt_sb[:, c, hs]), op=ALU.mult), reads=[xs_.name, "dt"], writes=[xdt[k].name])
                    P.op("pool", lambda e: e.tensor_tensor(out=v3(xdd[k][:]), in0=xg, in1=b8(dtdec[:, c, hs]), op=ALU.mult), reads=[xs_.name, "dtdec"], writes=[xdd[k].name])

                def mid(i):
                    x = ctx(i)
                    sc, cc, g, c, k, Bs, Cs, xs_, bk, z_, tsl, hs = (x[n] for n in ("sc", "cc", "g", "c", "k", "Bs", "Cs", "xs_", "bk", "z_", "tsl", "hs"))
                    P.op("pe", [lambda e, r=r: e.matmul(psY[:, r * 64:(r + 1) * 64], lhsT=MT[k][:, r, :], rhs=xdt[k][:, r * 64:(r + 1) * 64], start=True, stop=True) for r in range(8)],
                         reads=[MT[k].name, xdt[k].name], writes=["psY"])
                    P.op("pe", lambda e: e.matmul(psYo[:], lhsT=Cs[:, g, tsl], rhs=hbf[:, g, :], start=True, stop=True), reads=[Cs.name, ("hbf", g)], writes=["psYo"])
                    P.op("pe", lambda e: e.matmul(psSt[:], lhsT=bk[:, g * 128:(g + 1) * 128], rhs=xdd[k][:], start=True, stop=True), reads=[bk.name, xdd[k].name], writes=["psSt"])
                    P.op("dve", lambda e: e.tensor_tensor(out=v3(t1[k][:]), in0=v3(psYo[:]), in1=b8(expA[:, c, hs]), op=ALU.mult), reads=["expA"], writes=[t1[k].name, "psYo"])
                    P.op("dve", lambda e: e.tensor_tensor(out=t2[k][:], in0=psY[:], in1=t1[k][:], op=ALU.add), reads=[t1[k].name], writes=[t2[k].name, "psY"])
                    P.op("dve", lambda e: e.tensor_tensor(out=v3(hst[:, g, :]), in0=v3(hst[:, g, :]), in1=b8(cdec[:, c, hs]), op=ALU.mult), reads=[("hst", g), "cdec"], writes=[("hst", g)])
                    P.op("dve", lambda e: e.tensor_tensor(out=hst[:, g, :], in0=hst[:, g, :], in1=psSt[:], op=ALU.add), reads=[("hst", g)], writes=[("hst", g), "psSt"])
                    P.op("act", lambda e: e.activation(out=hbf[:, g, :], in_=hst[:, g, :], func=AF.Copy), reads=[("hst", g)], writes=[("hbf", g)])
                    P.op("pool", lambda e: e.tensor_tensor(out=t3[k][:], in0=xs_[:, g * 512:(g + 1) * 512], in1=dsk[:, g * 512:(g + 1) * 512], op=ALU.mult), reads=[xs_.name, "dsk"], writes=[t3[k].name])
                    P.op("pool", lambda e: e.tensor_tensor(out=t3[k][:], in0=t3[k][:], in1=t2[k][:], op=ALU.add), reads=[t3[k].name, t2[k].name], writes=[t3[k].name])
                    P.op("act", lambda e: e.activation(out=szt[k][:], in_=z_[:], func=AF.Silu), reads=[z_.name], writes=[szt[k].name])
                    P.op("dve", lambda e: e.tensor_tensor(out=yg[k][:], in0=t3[k][:], in1=szt[k][:], op=ALU.mult), reads=[t3[k].name, szt[k].name], writes=[yg[k].name])

                def tail(i):
                    x = ctx(i)
                    g, c, k, sa = x["g"], x["c"], x["k"], x["sa"]
                    P.op("act", lambda e: e.activation(out=junk[:], in_=yg[k][:], func=AF.Square, accum_out=sa[:, 0:1]), reads=[yg[k].name], writes=["junk", (sa.name, 0)])
                    P.op("dve", lambda e: e.tensor_scalar(out=sa[:, 1:2], in0=sa[:, 0:1], scalar1=1.0 / 512.0, scalar2=EPS, op0=ALU.mult, op1=ALU.add), reads=[(sa.name, 0)], writes=[(sa.name, 1)])
                    P.op("act", lambda e: e.activation(out=sa[:, 2:3], in_=sa[:, 1:2], func=AF.Sqrt), reads=[(sa.name, 1)], writes=[(sa.name, 2)])
                    P.op("dve", lambda e: e.reciprocal(out=sa[:, 3:4], in_=sa[:, 2:3]), reads=[(sa.name, 2)], writes=[(sa.name, 3)])
                    P.op("dve", lambda e: e.tensor_scalar(out=yn[k][:], in0=yg[k][:], scalar1=sa[:, 3:4], scalar2=None, op0=ALU.mult), reads=[yg[k].name, (sa.name, 3)], writes=[yn[k].name])
                    P.op("pe", [lambda e, q=q: e.transpose(psTr[:, q * 128:(q + 1) * 128], yn[k][:, q * 128:(q + 1) * 128], ident[:]) for q in range(4)], reads=[yn[k].name, "ident"], writes=["psTr"])
                    P.op("dve", lambda e: e.tensor_tensor(out=yts[k][:], in0=psTr[:].rearrange("p (q t) -> p q t", t=128), in1=snw[:, g * 4:(g + 1) * 4].unsqueeze(2).to_broadcast([128, 4, 128]), op=ALU.mult),
                         reads=["snw"], writes=[yts[k].name, "psTr"])
                    P.op("sp", lambda e: e.dma_start(out=yT_d[g * 512:(g + 1) * 512, c * 128:(c + 1) * 128].rearrange("(q p) t -> p q t", p=128), in_=yts[k][:]), reads=[yts[k].name], dma=True)

                for i in range(NIT + 2):
                    if i < NIT:
                        front(i)
                    if 1 <= i <= NIT:
                        mid(i - 1)
                    if i >= 2:
                        tail(i - 2)
                P.flush()

        def stage_D(l, src_d):
            with ExitStack() as st:
                def sb(name, shape, dt):
                    return st.enter_context(SBT("D_" + name, list(shape), dt))
                yTt = sb("yT", [128, 32, 512], BF16)
                atT = sb("atT", [128, 4, 512], BF16)
                mgT = sb("mgT", [128, 16, 512], BF16)
                wa = [sb("wa%d" % i, [128, 4, 256], BF16) for i in range(2)]
                ws = [sb("ws%d" % i, [128, 32, 256], BF16) for i in range(2)]
                wo = [sb("wo%d" % i, [128, 16, 256], BF16) for i in range(2)]
                ao = [sb("ao%d" % i, [128, 3, 4, 129], F32) for i in range(2)]
                E3 = [sb("E3%d" % i, [128, 3, 4], F32) for i in range(2)]
                Mx = [sb("Mx%d" % i, [128, 4], F32) for i in range(2)]
                Dn = [sb("Dn%d" % i, [128, 4], F32) for i in range(2)]
                acc = [sb("acc%d" % i, [128, 4, 128], F32) for i in range(2)]
                tmp = [sb("tmp%d" % i, [128, 4, 128], F32) for i in range(2)]
                atk = [sb("atk%d" % i, [128, 512], BF16) for i in range(2)]
                gat = [sb("ga%d" % i, [128, 512], BF16) for i in range(2)]
                gst_ = [sb("gs%d" % i, [128, 512], BF16) for i in range(2)]
                sga = [sb("sga%d" % i, [128, 512], F32) for i in range(2)]
                sgs = [sb("sgs%d" % i, [128, 512], F32) for i in range(2)]
                m1 = [sb("m1%d" % i, [128, 512], F32) for i in range(2)]
                m2 = [sb("m2%d" % i, [128, 512], F32) for i in range(2)]
                xt = [sb("xt%d" % i, [128, 512], F32) for i in range(2)]
                ps1 = [st.enter_context(PST("Dp1%d" % i, [128, 512], F32)) for i in range(2)]
                ps2 = [st.enter_context(PST("Dp2%d" % i, [128, 512], F32)) for i in range(2)]
                ps3 = [st.enter_context(PST("Dp3%d" % i, [128, 512], F32)) for i in range(2)]
                psT = st.enter_context(PST("DpT", [128, 512], BF16))
                wi = 0
                ei = 0
                ab = 0
                for tt in range(8):
                    tsl = slice(tt * 512, (tt + 1) * 512)
                    for hh in range(4):
                        P.op("sp", lambda e, hh=hh, tsl=tsl: e.dma_start(out=yTt[:, hh * 8:(hh + 1) * 8, :], in_=yT_d[hh * 1024:(hh + 1) * 1024, tsl].rearrange("(c p) t -> p c t", p=128)),
                             writes=[("yT", hh)], dma=True)
                    for q4 in range(4):
                        tb = tt * 4 + q4
                        a_, e3, mx, dn, ac, tm, ak = ao[ab % 2], E3[ab % 2], Mx[ab % 2], Dn[ab % 2], acc[ab % 2], tmp[ab % 2], atk[ab % 2]
                        ab += 1
                        P.op("sp", lambda e, a_=a_, tb=tb: e.dma_start(out=a_[:].rearrange("p g j c -> p g (j c)"), in_=ao_d[:, tb * 128:(tb + 1) * 128, :].rearrange("g p c -> p g c")),
                             writes=[a_.name], dma=True)
                        P.op("dve", lambda e, a_=a_, mx=mx: e.tensor_tensor(out=mx[:], in0=a_[:, 0, :, 128], in1=a_[:, 1, :, 128], op=ALU.max), reads=[a_.name], writes=[mx.name])
                        P.op("dve", lambda e, a_=a_, mx=mx: e.tensor_tensor(out=mx[:], in0=mx[:], in1=a_[:, 2, :, 128], op=ALU.max), reads=[a_.name, mx.name], writes=[mx.name])
                        P.op("dve", lambda e, a_=a_, mx=mx, e3=e3: e.tensor_tensor(out=e3[:], in0=a_[:, :, :, 128], in1=mx[:].unsqueeze(1).to_broadcast([128, 3, 4]), op=ALU.subtract),
                             reads=[a_.name, mx.name], writes=[e3.name])
                        P.op("act", lambda e, e3=e3: e.activation(out=e3[:], in_=e3[:], func=AF.Exp), reads=[e3.name], writes=[e3.name])
                        P.op("dve", lambda e, e3=e3, dn=dn: e.tensor_tensor(out=dn[:], in0=e3[:, 0, :], in1=e3[:, 1, :], op=ALU.add), reads=[e3.name], writes=[dn.name])
                        P.op("dve", lambda e, e3=e3, dn=dn: e.tensor_tensor(out=dn[:], in0=dn[:], in1=e3[:, 2, :], op=ALU.add), reads=[e3.name, dn.name], writes=[dn.name])
                        P.op("dve", lambda e, dn=dn: e.reciprocal(out=dn[:], in_=dn[:]), reads=[dn.name], writes=[dn.name])
                        P.op("dve", lambda e, e3=e3, dn=dn: e.tensor_tensor(out=e3[:], in0=e3[:], in1=dn[:].unsqueeze(1).to_broadcast([128, 3, 4]), op=ALU.mult),
                             reads=[e3.name, dn.name], writes=[e3.name])
                        bw = lambda e3, g: e3[:, g, :].unsqueeze(2).to_broadcast([128, 4, 128])
                        P.op("dve", lambda e, a_=a_, e3=e3, ac=ac: e.tensor_tensor(out=ac[:], in0=a_[:, 0, :, 0:128], in1=bw(e3, 0), op=ALU.mult), reads=[a_.name, e3.name], writes=[ac.name])
                        for g in (1, 2):
                            P.op("pool", lambda e, a_=a_, e3=e3, tm=tm, g=g: e.tensor_tensor(out=tm[:], in0=a_[:, g, :, 0:128], in1=bw(e3, g), op=ALU.mult),
                                 reads=[a_.name, e3.name], writes=[tm.name])
                            if g == 1:
                                P.op("dve", lambda e, ac=ac, tm=tm: e.tensor_tensor(out=ac[:], in0=ac[:], in1=tm[:], op=ALU.add), reads=[ac.name, tm.name], writes=[ac.name])
                            else:
                                P.op("dve", lambda e, ac=ac, tm=tm, ak=ak: e.tensor_tensor(out=ak[:].rearrange("p (j c) -> p j c", c=128), in0=ac[:], in1=tm[:], op=ALU.add),
                                     reads=[ac.name, tm.name], writes=[ak.name])
                        P.op("pe", [lambda e, ak=ak, j=j: e.transpose(psT[:, j * 128:(j + 1) * 128], ak[:, j * 128:(j + 1) * 128], ident[:]) for j in range(4)],
                             reads=[ak.name, "ident"], writes=["psT"])
                        P.op("act", lambda e, q4=q4: e.activation(out=atT[:, :, q4 * 128:(q4 + 1) * 128], in_=psT[:].rearrange("p (j t) -> p j t", t=128), func=AF.Copy),
                             reads=["psT"], writes=[("atT", q4)])
                    for cg in range(8):
                        wa_, ws_ = wa[wi % 2], ws[wi % 2]
                        wi += 1
                        csl = slice(cg * 256, (cg + 1) * 256)
                        P.op("pool", lambda e, wa_=wa_, csl=csl: e.dma_start(out=wa_[:], in_=wattn_in[l, :, csl].rearrange("(c p) n -> p c n", p=128)), writes=[wa_.name], dma=True)
                        for hh in range(2):
                            P.op("pool", lambda e, ws_=ws_, csl=csl, hh=hh: e.dma_start(out=ws_[:, hh * 16:(hh + 1) * 16, :], in_=wssm_in[l, hh * 2048:(hh + 1) * 2048, csl].rearrange("(c p) n -> p c n", p=128)),
                                 writes=[(ws_.name, hh)], dma=True)
                        for blk in range(2):
                            nb_ = cg * 2 + blk
                            k = ei % 2
                            ei += 1
                            P.op("sp", lambda e, k=k, nb_=nb_, tsl=tsl: e.dma_start(out=gat[k][:], in_=gaT_d[nb_ * 128:(nb_ + 1) * 128, tsl]), writes=[gat[k].name], dma=True)
                            P.op("sp", lambda e, k=k, nb_=nb_, tsl=tsl: e.dma_start(out=gst_[k][:], in_=gsT_d[nb_ * 128:(nb_ + 1) * 128, tsl]), writes=[gst_[k].name], dma=True)
                            P.op("pe", [lambda e, k=k, wa_=wa_, blk=blk, kc=kc: e.matmul(ps1[k][:], lhsT=wa_[:, kc, blk * 128:(blk + 1) * 128], rhs=atT[:, kc, :], start=(kc == 0), stop=(kc == 3)) for kc in range(4)],
                                 reads=[wa_.name] + [("atT", q) for q in range(4)], writes=[ps1[k].name])
                            P.op("pe", [lambda e, k=k, ws_=ws_, blk=blk, kc=kc: e.matmul(ps2[k][:], lhsT=ws_[:, kc, blk * 128:(blk + 1) * 128], rhs=yTt[:, kc, :], start=(kc == 0), stop=(kc == 31)) for kc in range(32)],
                                 reads=[(ws_.name, 0), (ws_.name, 1)] + [("yT", q) for q in range(4)], writes=[ps2[k].name])
                            P.op("act", lambda e, k=k: e.activation(out=sga[k][:], in_=gat[k][:], func=AF.Sigmoid), reads=[gat[k].name], writes=[sga[k].name])
                            P.op("act", lambda e, k=k: e.activation(out=sgs[k][:], in_=gst_[k][:], func=AF.Sigmoid), reads=[gst_[k].name], writes=[sgs[k].name])
                            P.op("dve", lambda e, k=k: e.tensor_tensor(out=m1[k][:], in0=ps1[k][:], in1=sga[k][:], op=ALU.mult), reads=[ps1[k].name, sga[k].name], writes=[m1[k].name])
                            P.op("dve", lambda e, k=k: e.tensor_tensor(out=m2[k][:], in0=ps2[k][:], in1=sgs[k][:], op=ALU.mult), reads=[ps2[k].name, sgs[k].name], writes=[m2[k].name])
                            P.op("pool", lambda e, k=k, nb_=nb_: e.tensor_tensor(out=mgT[:, nb_, :], in0=m1[k][:], in1=m2[k][:], op=ALU.add), reads=[m1[k].name, m2[k].name], writes=[("mgT", nb_)])
                    for cg in range(8):
                        wo_ = wo[wi % 2]
                        wi += 1
                        csl = slice(cg * 256, (cg + 1) * 256)
                        P.op("pool", lambda e, wo_=wo_, csl=csl: e.dma_start(out=wo_[:], in_=wout_in[l, :, csl].rearrange("(c p) n -> p c n", p=128)), writes=[wo_.name], dma=True)
                        for blk in range(2):
                            nb_ = cg * 2 + blk
                            k = ei % 2
                            ei += 1
                            P.op("sp", lambda e, k=k, nb_=nb_, tsl=tsl: e.dma_start(out=xt[k][:], in_=src_d[nb_ * 128:(nb_ + 1) * 128, tsl]), writes=[xt[k].name], dma=True)
                            P.op("pe", [lambda e, k=k, wo_=wo_, blk=blk, kc=kc: e.matmul(ps3[k][:], lhsT=wo_[:, kc, blk * 128:(blk + 1) * 128], rhs=mgT[:, kc, :], start=(kc == 0), stop=(kc == 15)) for kc in range(16)],
                                 reads=[wo_.name] + [("mgT", q) for q in range(16)], writes=[ps3[k].name])
                            P.op("dve", lambda e, k=k, nb_=nb_: e.scalar_tensor_tensor(out=xt[k][:], in0=ps3[k][:], scalar=modT[:, l, 32 + nb_:33 + nb_], in1=xt[k][:], op0=ALU.mult, op1=ALU.add),
                                 reads=[ps3[k].name, xt[k].name, "modT"], writes=[xt[k].name])
                            P.op("sp", lambda e, k=k, nb_=nb_, tsl=tsl: e.dma_start(out=xs_d[nb_ * 128:(nb_ + 1) * 128, tsl], in_=xt[k][:]), reads=[xt[k].name], dma=True)
                P.flush()

        def stage_E(l):
            with ExitStack() as st:
                def sb(name, shape, dt):
                    return st.enter_context(SBT("E_" + name, list(shape), dt))
                h2 = sb("h2", [128, 16, 512], BF16)
                actT = sb("actT", [128, 44, 512], BF16)
                wg = [sb("wg%d" % i, [128, 16, 256], BF16) for i in range(2)]
                wu = [sb("wu%d" % i, [128, 16, 256], BF16) for i in range(2)]
                wo2 = [sb("wo%d" % i, [128, 44, 256], BF16) for i in range(2)]
                sg = [sb("sg%d" % i, [128, 512], F32) for i in range(2)]
                xt = [sb("xt%d" % i, [128, 512], F32) for i in range(2)]
                pools = norm_pools(st)
                psg = [st.enter_context(PST("Epg%d" % i, [128, 512], F32)) for i in range(2)]
                psu = [st.enter_context(PST("Epu%d" % i, [128, 512], F32)) for i in range(2)]
                pso = [st.enter_context(PST("Epo%d" % i, [128, 512], F32)) for i in range(2)]
                wi = 0
                ei = 0
                for tt in range(8):
                    tsl = slice(tt * 512, (tt + 1) * 512)
                    norm_tiles(xs_d, a2[:, l, :], modT[:, l, 48:64], lambda i, t0, n, tt=tt: (h2[:, i, t0 - tt * 512:t0 - tt * 512 + n], ("h2", i)), tt * 512, 512, pools, "n2")
                    for fg in range(22):
                        wg_, wu_ = wg[wi % 2], wu[wi % 2]
                        wi += 1
                        P.op("pool", lambda e, wg_=wg_, fg=fg: e.dma_start(out=wg_[:], in_=wffi_in[l, :, fg * 256:(fg + 1) * 256].rearrange("(c p) n -> p c n", p=128)), writes=[wg_.name], dma=True)
                        P.op("pool", lambda e, wu_=wu_, fg=fg: e.dma_start(out=wu_[:], in_=wffi_in[l, :, DFF + fg * 256:DFF + (fg + 1) * 256].rearrange("(c p) n -> p c n", p=128)), writes=[wu_.name], dma=True)
                        for blk in range(2):
                            fb = fg * 2 + blk
                            k = ei % 2
                            ei += 1
                            hres = [("h2", i) for i in range(16)]
                            P.op("pe", [lambda e, k=k, wg_=wg_, blk=blk, kc=kc: e.matmul(psg[k][:], lhsT=wg_[:, kc, blk * 128:(blk + 1) * 128], rhs=h2[:, kc, :], start=(kc == 0), stop=(kc == 15)) for kc in range(16)],
                                 reads=[wg_.name] + hres, writes=[psg[k].name])
                            P.op("pe", [lambda e, k=k, wu_=wu_, blk=blk, kc=kc: e.matmul(psu[k][:], lhsT=wu_[:, kc, blk * 128:(blk + 1) * 128], rhs=h2[:, kc, :], start=(kc == 0), stop=(kc == 15)) for kc in range(16)],
                                 reads=[wu_.name] + hres, writes=[psu[k].name])
                            P.op("act", lambda e, k=k: e.activation(out=sg[k][:], in_=psg[k][:], func=AF.Silu), reads=[psg[k].name], writes=[sg[k].name])
                            P.op("dve", lambda e, k=k, fb=fb: e.tensor_tensor(out=actT[:, fb, :], in0=psu[k][:], in1=sg[k][:], op=ALU.mult), reads=[psu[k].name, sg[k].name], writes=[("actT", fb)])
                    for cg in range(8):
                        wo_ = wo2[wi % 2]
                        wi += 1
                        csl = slice(cg * 256, (cg + 1) * 256)
                        for hh in range(4):
                            P.op("pool", lambda e, wo_=wo_, csl=csl, hh=hh: e.dma_start(out=wo_[:, hh * 11:(hh + 1) * 11, :], in_=wffo_in[l, hh * 1408:(hh + 1) * 1408, csl].rearrange("(c p) n -> p c n", p=128)),
                                 writes=[(wo_.name, hh)], dma=True)
                        for blk in range(2):
                            nb_ = cg * 2 + blk
                            k = ei % 2
                            ei += 1
                            P.op("sp", lambda e, k=k, nb_=nb_, tsl=tsl: e.dma_start(out=xt[k][:], in_=xs_d[nb_ * 128:(nb_ + 1) * 128, tsl]), writes=[xt[k].name], dma=True)
                            P.op("pe", [lambda e, k=k, wo_=wo_, blk=blk, kc=kc: e.matmul(pso[k][:], lhsT=wo_[:, kc, blk * 128:(blk + 1) * 128], rhs=actT[:, kc, :], start=(kc == 0), stop=(kc == 43)) for kc in range(44)],
                                 reads=[(wo_.name, hh) for hh in range(4)] + [("actT", q) for q in range(44)], writes=[pso[k].name])
                            P.op("dve", lambda e, k=k, nb_=nb_: e.scalar_tensor_tensor(out=xt[k][:], in0=pso[k][:], scalar=modT[:, l, 80 + nb_:81 + nb_], in1=xt[k][:], op0=ALU.mult, op1=ALU.add),
                                 reads=[pso[k].name, xt[k].name, "modT"], writes=[xt[k].name])
                            P.op("sp", lambda e, k=k, nb_=nb_, tsl=tsl: e.dma_start(out=xs_d[nb_ * 128:(nb_ + 1) * 128, tsl], in_=xt[k][:]), reads=[xt[k].name], writes=[("xs", nb_ // 8, tt)], dma=True)
                P.flush()

        def stage_F():
            with ExitStack() as st:
                pools = norm_pools(st)
                ot = [st.enter_context(SBT("F_o%d" % i, [128, 16, 256], F32)) for i in range(2)]
                for ti in range(16):
                    o_ = ot[ti % 2]
                    norm_tiles(xs_d, fnw[:, :], zero16[:, :], lambda i, t0, n, o_=o_: (o_[:, i, :], (o_.name, i)), ti * 256, 256, pools, "nf")
                    for hh in range(2):
                        P.op("sp", lambda e, o_=o_, hh=hh, ti=ti: e.dma_start(out=outT[hh * 1024:(hh + 1) * 1024, ti * 256:(ti + 1) * 256].rearrange("(c p) t -> p c t", p=128), in_=o_[:, hh * 8:(hh + 1) * 8, :]),
                             reads=[(o_.name, i) for i in range(hh * 8, (hh + 1) * 8)], dma=True)
                P.flush()


        def stage_A(l, src_d):
                with ExitStack() as sa:
                    hT = sa.enter_context(SBT("hT", [128, 16, S], BF16))
                    with ExitStack() as sn:
                        pools = norm_pools(sn)
                        norm_tiles(src_d, a1[:, l, :], modT[:, l, 0:16], lambda i, t0, n: (hT[:, i, t0:t0 + n], ("hT", i, t0 // 512)), 0, S, pools, "n1")
                        if hT_dbg is not None and l == 0:
                            for i in range(16):
                                P.op("sp", lambda e, i=i: e.dma_start(out=hT_dbg[i * 128:(i + 1) * 128, :], in_=hT[:, i, :]),
                                     reads=[("hT", i, tt) for tt in range(8)], dma=True)
                        P.flush()
                    with ExitStack() as sw:
                        wb = [sw.enter_context(SBT("wb%d" % i, [128, 16, 512], BF16)) for i in range(2)]
                        stf = [sw.enter_context(SBT("stf%d" % i, [128, S], BF16)) for i in range(2)]
                        stt = [sw.enter_context(SBT("stt%d" % i, [128, 4, 512], BF16)) for i in range(2)]
                        stt32 = [sw.enter_context(SBT("stq%d" % i, [128, 4, 64], F32)) for i in range(2)]
                        psA = [sw.enter_context(PST("psA%d" % i, [128, 512], F32)) for i in range(6)]
                        segs = [("q", 0, 1536, "f", qT_d), ("k", 1536, 1536, "f", kT_d), ("v", 3072, 1536, "t", v_d),
                                ("z", 4608, 4096, "t", z_d), ("xbc", 8704, 6144, "f", xbcT_d), ("dt", 14848, 64, "t32", dtr_d),
                                ("ga", 14912, 2048, "f", gaT_d), ("gs", 16960, 2048, "f", gsT_d)]
                        wi = 0
                        pi = 0
                        fi = 0
                        ti_ = 0
                        ei = 0
                        for name, c0, ncols, mode, dest in segs:
                            for g0 in range(0, ncols, 512):
                                gw = min(512, ncols - g0)
                                wt = wb[wi % 2]
                                wi += 1
                                for hh in range(2):
                                    P.op("pool", lambda e, wt=wt, hh=hh, c0=c0, g0=g0, gw=gw, l=l: e.dma_start(
                                        out=wt[:, hh * 8:(hh + 1) * 8, 0:gw],
                                        in_=win_in[l, hh * 1024:(hh + 1) * 1024, c0 + g0:c0 + g0 + gw].rearrange("(c p) n -> p c n", p=128)),
                                        writes=[(wt.name, hh)], dma=True)
                                wres = [(wt.name, 0), (wt.name, 1)]
                                if mode == "f":
                                    for blk in range(gw // 128):
                                        sf = stf[fi % 2]
                                        fi += 1
                                        for tt in range(8):
                                            pt = psA[pi % 6]
                                            pi += 1
                                            P.op("pe", [lambda e, pt=pt, wt=wt, blk=blk, kc=kc, tt=tt: e.matmul(
                                                pt[:], lhsT=wt[:, kc, blk * 128:(blk + 1) * 128], rhs=hT[:, kc, tt * 512:(tt + 1) * 512],
                                                start=(kc == 0), stop=(kc == 15)) for kc in range(16)],
                                                reads=wres + [("hT", kc, tt) for kc in range(16)], writes=[pt.name])
                                            if ei % 2 == 0:
                                                P.op("act", lambda e, pt=pt, sf=sf, tt=tt: e.activation(out=sf[:, tt * 512:(tt + 1) * 512], in_=pt[:], func=AF.Copy),
                                                     reads=[pt.name], writes=[(sf.name, tt)])
                                            else:
                                                P.op("dve", lambda e, pt=pt, sf=sf, tt=tt: e.tensor_copy(out=sf[:, tt * 512:(tt + 1) * 512], in_=pt[:]),
                                                     reads=[pt.name], writes=[(sf.name, tt)])
                                            ei += 1
                                        r0 = g0 + blk * 128
                                        P.op("sp", lambda e, sf=sf, dest=dest, r0=r0: e.dma_start(out=dest[r0:r0 + 128, :], in_=sf[:]),
                                             reads=[(sf.name, tt) for tt in range(8)], dma=True)
                                else:
                                    for tb4 in range(8):
                                        st_ = (stt if mode == "t" else stt32)[ti_ % 2]
                                        ti_ += 1
                                        for q4 in range(4):
                                            tb = tb4 * 4 + q4
                                            pt = psA[pi % 6]
                                            pi += 1
                                            P.op("pe", [lambda e, pt=pt, wt=wt, kc=kc, tb=tb, gw=gw: e.matmul(
                                                pt[:, 0:gw], lhsT=hT[:, kc, tb * 128:(tb + 1) * 128], rhs=wt[:, kc, 0:gw],
                                                start=(kc == 0), stop=(kc == 15)) for kc in range(16)],
                                                reads=wres + [("hT", kc, tb // 4) for kc in range(16)], writes=[pt.name])
                                            if ei % 2 == 0:
                                                P.op("act", lambda e, pt=pt, st_=st_, q4=q4, gw=gw: e.activation(out=st_[:, q4, 0:gw], in_=pt[:, 0:gw], func=AF.Copy),
                                                     reads=[pt.name], writes=[(st_.name, q4)])
                                            else:
                                                P.op("dve", lambda e, pt=pt, st_=st_, q4=q4, gw=gw: e.tensor_copy(out=st_[:, q4, 0:gw], in_=pt[:, 0:gw]),
                                                     reads=[pt.name], writes=[(st_.name, q4)])
                                            ei += 1
                                        P.op("sp", lambda e, st_=st_, dest=dest, tb4=tb4, g0=g0, gw=gw: e.dma_start(
                                            out=dest[tb4 * 512:(tb4 + 1) * 512, g0:g0 + gw].rearrange("(q p) c -> p q c", p=128), in_=st_[:, :, 0:gw]),
                                            reads=[(st_.name, q4) for q4 in range(4)], dma=True)
                        P.flush()

        for l in range(nlayers):
            src_d = xT_in if l == 0 else xs_d
            if run("A"):
                stage_A(l, src_d)
            if run("B"):
                stage_B(l)
            if run("C") or run("C1") or run("C23"):
                stage_C(l)
            if run("D"):
                stage_D(l, src_d)
            if run("E"):
                stage_E(l)
        if run("F"):
            stage_F()
    return nc


def prep_shared(inputs):
    f = lambda a: np.ascontiguousarray(np.asarray(a, dtype=np.float32))
    sh = {}
    sh["rel_bias"] = f(inputs["rel_bias"])
    sh["n1w"] = f(inputs["norm1_w"].reshape(DEPTH, 16, 128).transpose(0, 2, 1))
    sh["n2w"] = f(inputs["norm2_w"].reshape(DEPTH, 16, 128).transpose(0, 2, 1))
    sh["fnw"] = f(inputs["final_norm_w"].reshape(16, 128).T)
    sh["w_mod"] = f(inputs["w_mod"])
    sh["bmodT"] = f(inputs["b_mod"].reshape(DEPTH, 96, 128).transpose(0, 2, 1))
    sh["w_in"] = f(inputs["w_in"])
    sh["convwT"] = f(inputs["conv_w"].reshape(DEPTH, 4, 48, 128).transpose(0, 3, 2, 1))
    sh["convbT"] = f(inputs["conv_b"].reshape(DEPTH, 48, 128).transpose(0, 2, 1))
    sh["dtb_b"] = f(np.broadcast_to(inputs["dt_bias"][:, None, :], (DEPTH, 128, 64)))
    sh["alog_b"] = f(np.broadcast_to(inputs["a_log"][:, None, :], (DEPTH, 128, 64)))
    sh["dsk_b"] = f(np.broadcast_to(np.repeat(inputs["d_skip"], 64, axis=1)[:, None, :], (DEPTH, 128, 4096)))
    sh["snwT"] = f(inputs["ssm_norm_w"].reshape(DEPTH, 32, 128).transpose(0, 2, 1))
    sh["w_attn_proj"] = f(inputs["w_attn_proj"])
    sh["w_ssm_proj"] = f(inputs["w_ssm_proj"])
    sh["w_out"] = f(inputs["w_out"])
    sh["w_ffn_in"] = f(inputs["w_ffn_in"])
    sh["w_ffn_out"] = f(inputs["w_ffn_out"])
    sh.update(host_consts())
    return sh


def prep_core(inputs, b, sh):
    m = dict(sh)
    m["xT"] = np.ascontiguousarray(np.asarray(inputs["x"][b], dtype=np.float32).T)
    m["cT"] = np.ascontiguousarray(np.asarray(inputs["c"][b], dtype=np.float32).reshape(16, 128).T)
    return m


_CACHE = {}


def kernel(**inputs):
    if "nc" not in _CACHE:
        _CACHE["nc"] = build()
    nc = _CACHE["nc"]
    sh = prep_shared(inputs)
    in_maps = [prep_core(inputs, i % 4, sh) for i in range(8)]
    res = run_bass_kernel_spmd(nc, in_maps, core_ids=list(range(8)))
    out = np.stack([np.ascontiguousarray(res.results[b]["outT"].T) for b in range(4)], axis=0)
    return out.astype(np.float32)
```
